# Optimizing a Trainium2 kernel written in Bass

```python
import math
import functools
import jax
import jax.numpy as jnp
from jax import lax
import numpy as np

D_MODEL = 1024
BATCH = 4
SEQ = 4096
DEPTH = 1
DEC_BATCH = 32
DEC_SEQ = 1
PAST_LEN = 8192
PAGE_SIZE = 128

H_A = 8
DH_A = 64
W_A = H_A * DH_A
H_IDX = 8
DH_IDX = 64
TOPK_MAX = 256
Q_BLOCK = 128
NUM_BUCKETS = 32
MAX_DISTANCE = 128
H_R = 4
DK_R = 128
DV_R = 128
W_R = H_R * DV_R
RET_CHUNK = 128
ROPE_BASE = 10000.0
MIX_WIDTH = W_A + W_R
D_FF = 2816
ALPHA = (2 * DEPTH) ** 0.25
BETA = (8 * DEPTH) ** -0.25
LN_EPS = 1e-5
GN_EPS = 1e-5
IN_SIZES = (W_A, W_A, W_A, H_IDX * DH_IDX, DH_IDX, H_IDX, H_R * DK_R, H_R * DK_R, W_R, W_R)
N_IN = 3 * W_A + H_IDX * DH_IDX + DH_IDX + H_IDX + 2 * H_R * DK_R + 2 * W_R

kernel_name = "hymba_dsa_retnet_macaron_deepnorm_step"


def _split_points():
    pts, acc = [], 0
    for s in IN_SIZES[:-1]:
        acc += s
        pts.append(acc)
    return pts


def layer_norm(x, g, b):
    xf = x.astype(jnp.float32)
    mu = xf.mean(-1, keepdims=True)
    var = jnp.square(xf - mu).mean(-1, keepdims=True)
    return ((xf - mu) * lax.rsqrt(var + LN_EPS)).astype(x.dtype) * g + b


def swiglu(x, wg, wu, wd):
    return (jax.nn.silu(x @ wg) * (x @ wu)) @ wd


def rotary(x, pos):
    half = x.shape[-1] // 2
    freqs = ROPE_BASE ** (-jnp.arange(half, dtype=jnp.float32) / half)
    ang = pos.astype(jnp.float32)[:, None] * freqs[None, :]
    cos = jnp.cos(ang)[None, :, None, :]
    sin = jnp.sin(ang)[None, :, None, :]
    xf = x.astype(jnp.float32)
    x1, x2 = xf[..., :half], xf[..., half:]
    return jnp.concatenate([x1 * cos - x2 * sin, x1 * sin + x2 * cos], axis=-1)


def t5_bucket(rel):
    n = jnp.maximum(rel, 0)
    max_exact = NUM_BUCKETS // 2
    nf = jnp.maximum(n, 1).astype(jnp.float32)
    large = max_exact + (jnp.log(nf / max_exact) / math.log(MAX_DISTANCE / max_exact)
                         * (NUM_BUCKETS - max_exact)).astype(jnp.int32)
    large = jnp.minimum(large, NUM_BUCKETS - 1)
    return jnp.where(n < max_exact, n, large)


def take_rows(arr, idx):
    return jax.vmap(lambda a, i: a[i])(arr, idx)


def indexer_scores(qi, ki, wi, qpos, kpos):
    dots = jnp.einsum('bqhd,bld->bqhl', qi, ki).astype(jnp.float32) * DH_IDX ** -0.5
    s = jnp.einsum('bqh,bqhl->bql', wi.astype(jnp.float32) * H_IDX ** -0.5, jax.nn.relu(dots))
    return jnp.where(kpos[None, None, :] <= qpos[None, :, None], s, -jnp.inf)


def sparse_attend(q, qpos, scores, topk, gather_kv, rel_bias):
    _, idx = lax.top_k(scores, topk)
    valid = idx <= qpos[None, :, None]
    k_sel, v_sel = gather_kv(idx)
    logits = jnp.einsum('bqhd,bqkhd->bhqk', q, k_sel).astype(jnp.float32) * DH_A ** -0.5
    bias = rel_bias[t5_bucket(qpos[None, :, None] - idx)].astype(jnp.float32)
    logits = logits + bias.transpose(0, 3, 1, 2)
    logits = jnp.where(valid[:, None], logits, -jnp.inf)
    p = jax.nn.softmax(logits, axis=-1)
    return jnp.einsum('bhqk,bqkhd->bqhd', p.astype(v_sel.dtype), v_sel).astype(q.dtype)


def prompt_attention(q, k, v, qi, ki, wi, rel_bias):
    B, S = q.shape[:2]
    topk = min(TOPK_MAX, S // 4)
    kpos = jnp.arange(S, dtype=jnp.int32)

    def gather_kv(idx):
        return take_rows(k, idx), take_rows(v, idx)

    def block(start):
        qb = lax.dynamic_slice_in_dim(q, start, Q_BLOCK, axis=1)
        qib = lax.dynamic_slice_in_dim(qi, start, Q_BLOCK, axis=1)
        wib = lax.dynamic_slice_in_dim(wi, start, Q_BLOCK, axis=1)
        qpos = start + jnp.arange(Q_BLOCK, dtype=jnp.int32)
        scores = indexer_scores(qib, ki, wib, qpos, kpos)
        return sparse_attend(qb, qpos, scores, topk, gather_kv, rel_bias)

    out = lax.map(block, jnp.arange(0, S, Q_BLOCK, dtype=jnp.int32))
    return out.transpose(1, 0, 2, 3, 4).reshape(B, S, H_A, DH_A)


def sample_attention(q, k_new, v_new, qi, ki_new, wi, rel_bias, cache_k, cache_v, cache_kidx, page_table, layer):
    DB, DS = q.shape[:2]
    n_pages = page_table.shape[1]
    past = n_pages * PAGE_SIZE
    L = past + DS
    topk = min(TOPK_MAX, L // 4)
    qpos = past + jnp.arange(DS, dtype=jnp.int32)
    kpos = jnp.arange(L, dtype=jnp.int32)
    ki_past = cache_kidx[layer, page_table].reshape(DB, past, DH_IDX)
    ki_all = jnp.concatenate([ki_past.astype(ki_new.dtype), ki_new], axis=1)
    scores = indexer_scores(qi, ki_all, wi, qpos, kpos)

    def gather_kv(idx):
        is_new = (idx >= past)[..., None, None]
        page = take_rows(page_table, jnp.minimum(idx // PAGE_SIZE, n_pages - 1))
        off = idx % PAGE_SIZE
        new_i = jnp.clip(idx - past, 0, DS - 1)
        k_sel = jnp.where(is_new, take_rows(k_new, new_i), cache_k[layer, page, off].astype(k_new.dtype))
        v_sel = jnp.where(is_new, take_rows(v_new, new_i), cache_v[layer, page, off].astype(v_new.dtype))
        return k_sel, v_sel

    return sparse_attend(q, qpos, scores, topk, gather_kv, rel_bias)


def retention(q, k, v, s0):
    B, S = q.shape[:2]
    C = RET_CHUNK if S % RET_CHUNK == 0 else S
    nc = S // C
    lg = jnp.log1p(-jnp.exp2(-5.0 - jnp.arange(H_R, dtype=jnp.float32)))
    i = jnp.arange(C, dtype=jnp.float32)
    diff = i[:, None] - i[None, :]
    causal = diff >= 0
    intra = jnp.where(causal[None], jnp.exp(jnp.where(causal, diff, 0.0)[None] * lg[:, None, None]), 0.0)
    cross_decay = jnp.exp((i[:, None] + 1.0) * lg[None, :])
    kv_decay = jnp.exp((C - 1.0 - i)[:, None] * lg[None, :])
    chunk_decay = jnp.exp(C * lg)

    def to_chunks(a):
        return a.reshape(B, nc, C, H_R, a.shape[-1]).transpose(1, 0, 2, 3, 4)

    def step(s, inp):
        qc, kc, vc = inp
        att = jnp.einsum('bihd,bjhd->bhij', qc, kc) * intra[None]
        o = (jnp.einsum('bhij,bjhv->bihv', att, vc)
             + jnp.einsum('bihd,bhdv->bihv', qc, s) * cross_decay[None, :, :, None])
        s_new = chunk_decay[None, :, None, None] * s + jnp.einsum('bjhd,jh,bjhv->bhdv', kc, kv_decay, vc)
        return s_new, o

    s_fin, o = lax.scan(step, s0, (to_chunks(q), to_chunks(k), to_chunks(v)))
    return o.transpose(1, 0, 2, 3, 4).reshape(B, S, H_R, DV_R), s_fin


def decoder_layer(x, pos, attend, s0, ffn1_wg, ffn1_wu, ffn1_wd, ln1_g, ln1_b, w_in, ret_gn_g, w_out,
                  ln2_g, ln2_b, ffn2_wg, ffn2_wu, ffn2_wd, ln3_g, ln3_b):
    B, S, _ = x.shape
    h = layer_norm(ALPHA * x + 0.5 * swiglu(x, ffn1_wg, ffn1_wu, ffn1_wd), ln1_g, ln1_b)
    q_a, k_a, v_a, q_i, k_i, w_i, q_r, k_r, v_r, g_r = jnp.split(h @ w_in, _split_points(), axis=-1)
    q_a = q_a.reshape(B, S, H_A, DH_A)
    k_a = k_a.reshape(B, S, H_A, DH_A)
    v_a = v_a.reshape(B, S, H_A, DH_A)
    q_i = q_i.reshape(B, S, H_IDX, DH_IDX)
    o_a = attend(q_a, k_a, v_a, q_i, k_i, w_i)
    qr = rotary(q_r.reshape(B, S, H_R, DK_R), pos)
    kr = rotary(k_r.reshape(B, S, H_R, DK_R), pos) * DK_R ** -0.5
    vr = v_r.reshape(B, S, H_R, DV_R).astype(jnp.float32)
    o_r, s_new = retention(qr, kr, vr, s0.astype(jnp.float32))
    mu = o_r.mean(-1, keepdims=True)
    var = jnp.square(o_r - mu).mean(-1, keepdims=True)
    y_r = ((o_r - mu) * lax.rsqrt(var + GN_EPS)).reshape(B, S, W_R).astype(x.dtype) * ret_gn_g
    y_r = jax.nn.silu(g_r) * y_r
    mix = jnp.concatenate([o_a.reshape(B, S, W_A), y_r], axis=-1) @ w_out
    h = layer_norm(ALPHA * h + mix, ln2_g, ln2_b)
    h = layer_norm(ALPHA * h + 0.5 * swiglu(h, ffn2_wg, ffn2_wu, ffn2_wd), ln3_g, ln3_b)
    return h, k_a, v_a, k_i, s_new


def setup_inputs(seed: int = 0) -> dict:
    key = jax.random.key(seed)
    ks = jax.random.split(key, 32)
    f32 = jnp.float32
    nrm = lambda k, shp, sc: jax.random.normal(k, shp, f32) * sc
    n_pages = PAST_LEN // PAGE_SIZE
    n_used = DEC_BATCH * n_pages
    n_pool = n_used + n_used // 4
    perm = jax.random.permutation(ks[0], n_pool)
    page_table = perm[:n_used].reshape(DEC_BATCH, n_pages).astype(jnp.int32)

    w_in = nrm(ks[8], (DEPTH, D_MODEL, N_IN), D_MODEL ** -0.5)
    w_in = w_in.at[..., 2 * W_A:3 * W_A].multiply(BETA).at[..., N_IN - 2 * W_R:N_IN - W_R].multiply(BETA)

    return {
        "x_prompt": nrm(ks[1], (BATCH, SEQ, D_MODEL), 1.0),
        "x_sample": nrm(ks[2], (DEC_BATCH, DEC_SEQ, D_MODEL), 1.0),
        "cache_k": nrm(ks[3], (DEPTH, n_pool, PAGE_SIZE, H_A, DH_A), 1.0),
        "cache_v": nrm(ks[4], (DEPTH, n_pool, PAGE_SIZE, H_A, DH_A), BETA),
        "cache_kidx": nrm(ks[5], (DEPTH, n_pool, PAGE_SIZE, DH_IDX), 1.0),
        "state_ret": nrm(ks[6], (DEPTH, DEC_BATCH, H_R, DK_R, DV_R), 0.5),
        "page_table": page_table,
        "rel_bias": nrm(ks[7], (NUM_BUCKETS, H_A), 0.5),
        "ffn1_wg": nrm(ks[9], (DEPTH, D_MODEL, D_FF), D_MODEL ** -0.5),
        "ffn1_wu": nrm(ks[10], (DEPTH, D_MODEL, D_FF), D_MODEL ** -0.5),
        "ffn1_wd": nrm(ks[11], (DEPTH, D_FF, D_MODEL), BETA * D_FF ** -0.5),
        "ln1_g": 1.0 + nrm(ks[12], (DEPTH, D_MODEL), 0.05),
        "ln1_b": nrm(ks[13], (DEPTH, D_MODEL), 0.02),
        "w_in": w_in,
        "ret_gn_g": 1.0 + nrm(ks[14], (DEPTH, W_R), 0.05),
        "w_out": nrm(ks[15], (DEPTH, MIX_WIDTH, D_MODEL), BETA * MIX_WIDTH ** -0.5),
        "ln2_g": 1.0 + nrm(ks[16], (DEPTH, D_MODEL), 0.05),
        "ln2_b": nrm(ks[17], (DEPTH, D_MODEL), 0.02),
        "ffn2_wg": nrm(ks[18], (DEPTH, D_MODEL, D_FF), D_MODEL ** -0.5),
        "ffn2_wu": nrm(ks[19], (DEPTH, D_MODEL, D_FF), D_MODEL ** -0.5),
        "ffn2_wd": nrm(ks[20], (DEPTH, D_FF, D_MODEL), BETA * D_FF ** -0.5),
        "ln3_g": 1.0 + nrm(ks[21], (DEPTH, D_MODEL), 0.05),
        "ln3_b": nrm(ks[22], (DEPTH, D_MODEL), 0.02),
    }


def reference(x_prompt, x_sample, cache_k, cache_v, cache_kidx, state_ret, page_table, rel_bias,
              ffn1_wg, ffn1_wu, ffn1_wd, ln1_g, ln1_b, w_in, ret_gn_g, w_out,
              ln2_g, ln2_b, ffn2_wg, ffn2_wu, ffn2_wd, ln3_g, ln3_b):
    B, S, _ = x_prompt.shape
    DB, DS, _ = x_sample.shape
    past = page_table.shape[1] * PAGE_SIZE
    pos_p = jnp.arange(S, dtype=jnp.int32)
    pos_s = past + jnp.arange(DS, dtype=jnp.int32)
    s0_p = jnp.zeros((B, H_R, DK_R, DV_R), jnp.float32)
    attend_p = functools.partial(prompt_attention, rel_bias=rel_bias)

    hp, hs = x_prompt, x_sample
    kp_l, vp_l, kip_l, sp_l = [], [], [], []
    ks_l, vs_l, kis_l, ss_l = [], [], [], []
    for l in range(DEPTH):
        lw = (ffn1_wg[l], ffn1_wu[l], ffn1_wd[l], ln1_g[l], ln1_b[l], w_in[l], ret_gn_g[l], w_out[l],
              ln2_g[l], ln2_b[l], ffn2_wg[l], ffn2_wu[l], ffn2_wd[l], ln3_g[l], ln3_b[l])
        attend_s = functools.partial(sample_attention, rel_bias=rel_bias, cache_k=cache_k, cache_v=cache_v,
                                     cache_kidx=cache_kidx, page_table=page_table, layer=l)
        hp, kp, vp, kip, sp = decoder_layer(hp, pos_p, attend_p, s0_p, *lw)
        hs, ksn, vsn, kisn, ssn = decoder_layer(hs, pos_s, attend_s, state_ret[l], *lw)
        kp_l.append(kp); vp_l.append(vp); kip_l.append(kip); sp_l.append(sp)
        ks_l.append(ksn); vs_l.append(vsn); kis_l.append(kisn); ss_l.append(ssn)

    k_prompt = jnp.stack(kp_l)
    v_prompt = jnp.stack(vp_l)
    kidx_prompt = jnp.stack(kip_l)
    ret_state_prompt = jnp.stack(sp_l)
    k_sample = jnp.stack(ks_l)
    v_sample = jnp.stack(vs_l)
    kidx_sample = jnp.stack(kis_l)
    ret_state_sample = jnp.stack(ss_l)
    return (hp, hs, k_prompt, v_prompt, kidx_prompt, ret_state_prompt,
            k_sample, v_sample, kidx_sample, ret_state_sample)
```

```python
import numpy as np
from contextlib import ExitStack
import concourse.bass as bass
import concourse.mybir as mybir
from concourse.bass_utils import run_bass_kernel_spmd

F32 = mybir.dt.float32
BF16 = mybir.dt.bfloat16
I32 = mybir.dt.int32
ALU = mybir.AluOpType
AF = mybir.ActivationFunctionType
AX = mybir.AxisListType
ESZ = {F32: 4, BF16: 2, I32: 4}

GRAN = 512
ENGS = ("pe", "act", "dve", "pool", "sp")
NDSEM = 8


class Op:
    __slots__ = ("eng", "fn", "deps", "is_dma", "sig", "signo", "dsem", "dval", "idx")


class Prog:
    def __init__(self, nc):
        self.nc = nc
        self.ops = {e: [] for e in ENGS}
        self.lastw = {}
        self.readers = {}
        self.dmas = {e: [] for e in ENGS}
        self.tracked_dram = set()

    def ap_keys(self, ap):
        name = ap.tensor.name
        sp = str(ap.space)
        if "DRAM" in sp.upper() or "HBM" in sp.upper():
            if name in self.tracked_dram:
                return [(name, 0)]
            return []
        if sp != "SB":
            return [(name, 0)]
        pat = ap.ap
        ps = pat[0][0]
        off = ap.offset % ps if ps > 0 else ap.offset
        ext = 1
        for st, cnt in pat[1:]:
            ext += (cnt - 1) * abs(st)
        es = ESZ[ap.dtype]
        lo = off * es
        hi = (off + ext) * es
        return [(name, g) for g in range(lo // GRAN, (hi - 1) // GRAN + 1)]

    def add(self, eng, fn, reads=(), writes=(), rkeys=(), wkeys=(), dma=False, waw=True):
        op = Op()
        op.eng = eng
        op.fn = fn
        op.is_dma = dma
        op.sig = False
        op.signo = 0
        op.idx = len(self.ops[eng])
        deps = {}
        rk = list(rkeys)
        for ap in reads:
            rk += self.ap_keys(ap)
        wk = list(wkeys)
        for ap in writes:
            wk += self.ap_keys(ap)

        def adddep(d):
            if d is op:
                return
            if d.is_dma:
                deps[("d", id(d))] = d
            else:
                k = ("c", d.eng)
                if k not in deps or deps[k].idx < d.idx:
                    deps[k] = d

        for k in rk:
            for d in self.lastw.get(k, ()):
                adddep(d)
            if k[0].startswith("ps"):
                r = self.readers.get(k)
                if r:
                    for e2, d in r[0].items():
                        if e2 != eng:
                            adddep(d)
        for k in wk:
            r = self.readers.get(k)
            if r:
                for d in r[0].values():
                    adddep(d)
                for d in r[1]:
                    adddep(d)
            if waw:
                for d in self.lastw.get(k, ()):
                    adddep(d)
        for k in rk:
            r = self.readers.get(k)
            if r is None:
                r = self.readers[k] = ({}, [])
            if dma:
                r[1].append(op)
            else:
                r[0][eng] = op
        for k in wk:
            if waw:
                self.lastw[k] = [op]
            else:
                self.lastw.setdefault(k, []).append(op)
            self.readers[k] = ({}, [])
        if dma:
            lst = self.dmas[eng]
            n = len(lst)
            op.dsem = n % NDSEM
            op.dval = 16 * (n // NDSEM + 1)
            if n >= NDSEM:
                adddep(lst[n - NDSEM])
            lst.append(op)
        op.deps = list(deps.values())
        self.ops[eng].append(op)
        return op

    def mm(self, out, lhsT, rhs, start=True, stop=True, **kw):
        return self.add("pe", lambda e: e.matmul(out, lhsT, rhs, start=start, stop=stop, **kw),
                        reads=[lhsT, rhs], writes=[out])

    def tr(self, out, in_, ident):
        return self.add("pe", lambda e: e.transpose(out, in_, ident), reads=[in_, ident], writes=[out])

    def act(self, out, in_, func, bias=None, scale=None, eng="act"):
        kw = {}
        rd = [in_]
        if bias is not None:
            kw["bias"] = bias
            if not isinstance(bias, (int, float)):
                rd.append(bias)
        if scale is not None:
            kw["scale"] = scale
            if not isinstance(scale, (int, float)):
                rd.append(scale)
        return self.add(eng, lambda e: e.activation(out, in_, func, **kw), reads=rd, writes=[out])

    def ts(self, eng, out, in0, s1, s2, op0, op1=None, accum_out=None):
        rd = [in0]
        for s in (s1, s2):
            if s is not None and not isinstance(s, (int, float)):
                rd.append(s)
        wr = [out]
        kw = {}
        if op1 is not None:
            kw["op1"] = op1
        if accum_out is not None:
            kw["accum_out"] = accum_out
            wr.append(accum_out)
        return self.add(eng, lambda e: e.tensor_scalar(out, in0, s1, s2, op0, **kw), reads=rd, writes=wr)

    def tt(self, eng, out, in0, in1, op):
        return self.add(eng, lambda e: e.tensor_tensor(out, in0, in1, op), reads=[in0, in1], writes=[out])

    def stt(self, eng, out, in0, scalar, in1, op0, op1):
        rd = [in0, in1]
        if not isinstance(scalar, (int, float)):
            rd.append(scalar)
        return self.add(eng, lambda e: e.scalar_tensor_tensor(out, in0, scalar, in1, op0, op1),
                        reads=rd, writes=[out])

    def copy(self, eng, out, in_):
        if eng == "act":
            return self.add(eng, lambda e: e.copy(out, in_), reads=[in_], writes=[out])
        return self.add(eng, lambda e: e.tensor_copy(out, in_), reads=[in_], writes=[out])

    def memset(self, eng, ap, val):
        return self.add(eng, lambda e: e.memset(ap, val), writes=[ap])

    def dma(self, q, out, in_, waw=True, **kw):
        return self.add(q, lambda e: e.dma_start(out, in_, **kw), reads=[in_], writes=[out], dma=True, waw=waw)

    def emit(self):
        nc = self.nc
        for e in ENGS:
            for op in self.ops[e]:
                for d in op.deps:
                    d.sig = True
        for e in ENGS:
            c = 0
            for op in self.ops[e]:
                if (not op.is_dma) and op.sig:
                    c += 1
                    op.signo = c
        with ExitStack() as es:
            csem = {e: es.enter_context(nc.semaphore("c_" + e)) for e in ENGS}
            dsem = {e: [es.enter_context(nc.semaphore("d_%s_%d" % (e, i))) for i in range(NDSEM)]
                    for e in ENGS if self.dmas[e]}
            block = es.enter_context(nc.Block())

            def run(e, eng):
                waited = {}
                for op in self.ops[e]:
                    need = {}
                    for d in op.deps:
                        if d.is_dma:
                            key = (d.eng, d.dsem)
                            val = d.dval
                        else:
                            if d.eng == "pe" and e == "pe":
                                continue
                            key = (d.eng, -1)
                            val = d.signo
                        if need.get(key, 0) < val:
                            need[key] = val
                    for key, val in need.items():
                        if waited.get(key, 0) < val:
                            sem = csem[key[0]] if key[1] < 0 else dsem[key[0]][key[1]]
                            eng.wait_ge(sem, val)
                            waited[key] = val
                    ins = op.fn(eng)
                    if op.is_dma:
                        ins.then_inc(dsem[e][op.dsem], 16)
                    elif op.sig:
                        ins.then_inc(csem[e], 1)
                lst = self.dmas[e]
                if lst:
                    fin = {}
                    for d in lst:
                        fin[d.dsem] = max(fin.get(d.dsem, 0), d.dval)
                    for i, v in fin.items():
                        if waited.get((e, i), 0) < v:
                            eng.wait_ge(dsem[e][i], v)

            @block.tensor
            def _(eng):
                run("pe", eng)

            @block.scalar
            def _(eng):
                run("act", eng)

            @block.vector
            def _(eng):
                run("dve", eng)

            @block.gpsimd
            def _(eng):
                run("pool", eng)

            @block.sync
            def _(eng):
                run("sp", eng)


import math

D = 1024
DFF = 2816
NFC = DFF // 128
NIN = 4168
T = 512
NB = T // 128
HALF = 2048
SEQ = 4096
NT = HALF // T
ALPHA = 2.0 ** 0.25
EPS = 1e-5
WSCALE = (64 ** -0.5) * (8 ** -0.5)
R0 = 16.0
NIT = 18
NEG = -1.0e30
MNEG = -30000.0
ARENA = 65536
OUTQ = "sp"

C_QA, C_KA, C_VA, C_QI, C_KI, C_WI, C_QR, C_KR, C_VR, C_GR = 0, 512, 1024, 1536, 2048, 2112, 2120, 2632, 3144, 3656
FM_CHUNKS = [C_QA + 128 * i for i in range(4)] + [C_KA + 128 * i for i in range(4)] + \
            [C_QI + 128 * i for i in range(4)] + [C_KI]
FM_W = [128] * 12 + [64]
TM_GROUPS = [(C_KA, 512), (C_VA, 512), (C_KI, 72), (C_QR, 512), (C_KR, 512), (C_VR, 512), (C_GR, 512)]

LG = [math.log1p(-2.0 ** (-5.0 - h)) for h in range(4)]
CDK = [math.exp(128.0 * LG[h]) for h in range(4)]


def host_consts():
    c = {}
    c["ident"] = np.eye(128, dtype=np.float32)
    t = np.arange(128)
    c["tri"] = np.where(t[None, :] <= t[:, None], 0.0, NEG).astype(np.float32)
    c["jmat"] = np.eye(128, dtype=np.float32)[::-1].copy()
    n = np.maximum(np.arange(384) - 127, 0)
    nf = np.maximum(n, 1).astype(np.float32)
    large = 16 + (np.log(nf / np.float32(16)) / np.float32(math.log(8.0)) * np.float32(16)).astype(np.int32)
    bk = np.where(n < 16, n, np.minimum(large, 31))
    oh = np.zeros((32, 384), np.float32)
    oh[bk, np.arange(384)] = 1.0
    oh[31, :] -= 1.0
    c["ohp"] = oh
    i = np.arange(128)
    intra = np.zeros((128, 4, 128), np.float64)
    cdt = np.zeros((128, 4, 128), np.float64)
    kvd = np.zeros((128, 4, 128), np.float64)
    for h in range(4):
        diff = i[None, :] - i[:, None]
        intra[:, h, :] = np.where(diff >= 0, np.exp(np.maximum(diff, 0) * LG[h]), 0.0)
        cdt[:, h, :] = np.exp((i[None, :] + 1.0) * LG[h])
        kvd[:, h, :] = np.exp((127.0 - i[:, None]) * LG[h])
    c["intra"] = intra.reshape(128, 512).astype(np.float32)
    c["cdt"] = cdt.reshape(128, 512).astype(np.float32)
    c["kvd"] = kvd.reshape(128, 512).astype(np.float32)
    p = np.arange(128)
    c["ltri"] = (p[:, None] < p[None, :]).astype(np.float32)
    c["iotaq"] = np.tile(np.arange(1, 257, dtype=np.float32)[None, :], (128, 1))
    cc = np.arange(64)
    rowtab = np.zeros((128, 65), np.float32)
    rowtab[:, :64] = 2 * cc[None, :] + (p[:, None] // 64)
    lstab = np.zeros((128, 65), np.float32)
    lstab[:, :64] = (p[:, None] % 64) * 128 + rowtab[:, :64]
    lstab[:, 64] = 8192.0
    c["rowtab"] = rowtab
    c["lstab"] = lstab
    nn = np.arange(0, 8193)
    nf = np.maximum(nn, 1).astype(np.float32)
    large = 16 + (np.log(nf / np.float32(16)) / np.float32(math.log(8.0)) * np.float32(16)).astype(np.int32)
    bk = np.where(nn < 16, nn, np.minimum(large, 31))
    tb = np.array([float(np.argmax(bk >= b)) for b in range(32)], np.float32)
    c["tbt"] = np.tile(tb[None, :], (128, 1))
    c["rope_s"] = rope_table(np.full(128, 8192))
    return c


def rope_table(pos):
    half = 64
    freqs = (np.float32(10000.0) ** (-np.arange(half, dtype=np.float32) / np.float32(half))).astype(np.float32)
    ang = pos.astype(np.float32)[:, None] * freqs[None, :]
    co = np.cos(ang).astype(np.float32)
    si = np.sin(ang).astype(np.float32)
    s = np.float32(128.0 ** -0.5)
    return np.stack([co, si, co * s, si * s], axis=1).astype(np.float32)


def build(nc, debug=False):
    def din(name, shape, dt=F32):
        return nc.dram_tensor(name, list(shape), dt, kind="ExternalInput").ap()

    def dout(name, shape, dt=F32):
        return nc.dram_tensor(name, list(shape), dt, kind="ExternalOutput").ap()

    xpre = din("xpre", [HALF, D])
    xown = din("xown", [HALF, D])
    ident_d = din("ident", [128, 128]); tri_d = din("tri", [128, 128]); jmat_d = din("jmat", [128, 128])
    ohp_d = din("ohp", [32, 384]); intra_d = din("intra", [128, 512]); cdt_d = din("cdt", [128, 512])
    kvd_d = din("kvd", [128, 512]); kvdp_d = din("kvd_pre", [128, 512]); pbias_d = din("pbias", [128, 1])
    rope_own = din("rope_own", [HALF, 4, 64]); rope_pre = din("rope_pre", [HALF, 4, 64])
    relb = din("rel_bias", [32, 8])
    w1g = din("ffn1_wg", [D, DFF]); w1u = din("ffn1_wu", [D, DFF]); w1d = din("ffn1_wd", [DFF, D])
    w2g = din("ffn2_wg", [D, DFF]); w2u = din("ffn2_wu", [D, DFF]); w2d = din("ffn2_wd", [DFF, D])
    win = din("w_in", [D, NIN]); wout = din("w_out", [D, D])
    ln1g = din("ln1_g", [D]); ln1b = din("ln1_b", [D])
    ln2g = din("ln2_g", [D]); ln2b = din("ln2_b", [D])
    ln3g = din("ln3_g", [D]); ln3b = din("ln3_b", [D])
    gng_d = din("ret_gn_g", [512])

    xs_d = din("xs", [128, D]); pt_d = din("pt_s", [4, 64], I32); state_d = din("state_s", [4, 4, 128, 128])
    ckidx_d = din("cache_kidx", [2560 * 4, 2048]); ck_d = din("cache_k", [2560 * 128, 512]); cv_d = din("cache_v", [2560 * 128, 512])
    rope_s = din("rope_s", [128, 4, 64]); ltri_d = din("ltri", [128, 128]); iotaq_d = din("iotaq", [128, 256])
    rowtab_d = din("rowtab", [128, 65]); lstab_d = din("lstab", [128, 65]); tbt_d = din("tbt", [128, 32])
    o_ys = dout("o_ys", [4, D]); o_ks = dout("o_ks", [4, 512]); o_vs = dout("o_vs", [4, 512]); o_kis = dout("o_kis", [4, 64])
    o_Ss = dout("o_Ss", [4, 128, 512])
    o_y = dout("o_y", [HALF, D])
    o_k = dout("o_k", [HALF, 512]); o_v = dout("o_v", [HALF, 512]); o_ki = dout("o_ki", [HALF, 64])
    o_S = dout("o_S", [128, 512])
    if debug:
        d_mixA = dout("d_mixA", [NT, 64, 8 * T], BF16)
        d_mixR = dout("d_mixR", [NT, 128, 4 * T], BF16)
        d_sc = dout("d_sc", [128, SEQ])
        d_thr = dout("d_thr", [NT * NB, 128, 7])

    def scr(name, shape, dt=BF16):
        return nc.dram_tensor(name, list(shape), dt, kind="Internal").ap()

    s1g = scr("s1g", [NFC, 128, 8, 128]); s1u = scr("s1u", [NFC, 128, 8, 128]); s1d = scr("s1d", [NFC, 128, D])
    s2g = scr("s2g", [NFC, 128, 8, 128]); s2u = scr("s2u", [NFC, 128, 8, 128]); s2d = scr("s2d", [NFC, 128, D])
    sfm = scr("sfm", [13, 128, 8, 128]); stm = scr("stm", [7, 128, 8, 512])
    swoA = scr("swoA", [64, 8, D]); swoR = scr("swoR", [128, 4, D])
    fscr = scr("fscr", [8, 384], F32)

    P = Prog(nc)
    for a in (s1g, s1u, s1d, s2g, s2u, s2d, sfm, stm, swoA, swoR, fscr):
        P.tracked_dram.add(a.tensor.name)

    with ExitStack() as es:
        def sb(name, shape, dt=F32):
            return es.enter_context(nc.sbuf_tensor("sb_" + name, list(shape), dt))

        ps = [es.enter_context(nc.psum_tensor("ps%d" % i, [128, 512], F32)) for i in range(8)]
        arena = sb("arena", [128, ARENA // 4])

        def carve(off, nbytes, dt=F32, parts=128):
            a = arena[0:parts, off // 4:(off + nbytes) // 4]
            return a.bitcast(dt) if dt != F32 else a

        xT = carve(0, 8192, BF16).rearrange("p (k t) -> p k t", k=8)
        actb = carve(8192, 22528, BF16).rearrange("p (c t) -> p c t", c=NFC)
        sg = [carve(30720 + 2048 * i, 2048) for i in range(2)]
        wgb = [carve(34816 + 2048 * i, 2048, BF16).rearrange("p (k f) -> p k f", k=8) for i in range(3)]
        wub = [carve(40960 + 2048 * i, 2048, BF16).rearrange("p (k f) -> p k f", k=8) for i in range(3)]
        wdr = [carve(47104 + 2048 * i, 2048, BF16) for i in range(3)]
        stages = [carve(0, 16672), carve(16896, 16672)]
        stagebs = [carve(33792, 8336, BF16), carve(42240, 8336, BF16)]
        hT = xT
        fmw = [carve(8192 + 2048 * i, 2048, BF16).rearrange("p (k f) -> p k f", k=8) for i in range(3)]
        tmw = carve(14336, 8192, BF16).rearrange("p (k f) -> p k f", k=8)
        ost = [carve(22528 + 2048 * i, 2048) for i in range(3)]
        qrot = carve(28672, 4096, BF16).rearrange("p (b f) -> p b f", b=NB)
        krot = carve(32768, 4096, BF16).rearrange("p (b f) -> p b f", b=NB)
        vrb = carve(36864, 4096, BF16).rearrange("p (b f) -> p b f", b=NB)
        sgr = carve(40960, 8192).rearrange("p (b f) -> p b f", b=NB)
        rx = carve(49152, 2048)
        tmp = [carve(51200 + 1024 * i, 1024) for i in range(4)]
        qkT = carve(55296, 2048, BF16)
        qpT = carve(57344, 1024, BF16)
        attm = carve(58368, 1024, BF16)
        kd = carve(59392, 1024, BF16)
        yn = carve(60416, 2048)
        ytm = carve(62464, 1024, BF16)
        ropes = [carve(63488 + 1024 * i, 1024).rearrange("p (a f) -> p a f", a=4) for i in range(2)]
        sc = carve(0, 16384)
        MB = [carve(16384 + 8192 * i, 8192, BF16) for i in range(4)]
        Rr = [carve(49152 + 1024 * i, 1024, BF16) for i in range(3)]
        Dg = carve(52224, 2048, BF16).rearrange("p (h t) -> p h t", h=8)
        pT = [carve(54272 + 1024 * i, 1024, BF16) for i in range(4)]
        osb = carve(58368, 2048)
        rden = carve(60416, 2048)
        ohp = carve(51200, 1536, F32, parts=32)
        Fs = carve(52736, 1536, F32, parts=8)
        hk = [carve(54272 + 512 * i, 512) for i in range(2)]
        rbt = carve(55296, 32, F32, parts=32)
        woA = carve(0, 16384, BF16, parts=64).rearrange("p (h n) -> p h n", h=8)
        woR = carve(16384, 8192, BF16).rearrange("p (h n) -> p h n", h=4)

        ident = sb("identf", [128, 128]); identb = sb("identb", [128, 128], BF16)
        tri = sb("tri", [128, 128]); jmat = sb("jmat", [128, 128]); onesf = sb("onesf", [128, 128])
        GB = sb("GB", [128, 2, D])
        gng = sb("gng", [128, 512])
        xtm = sb("xtm", [128, NB, D])
        kT = sb("kT", [128, 4, SEQ], BF16)
        kiT2 = sb("kiT2", [128, SEQ], BF16)
        vflat = sb("vaug", [128, (SEQ // 128) * 8 * 65 + 64], BF16)
        vaug = vflat[:, 0:(SEQ // 128) * 8 * 65].rearrange("p (j h c) -> p j h c", j=SEQ // 128, h=8)
        qT = sb("qT", [128, 4, T], BF16); qiT = sb("qiT", [128, 4, T], BF16)
        wi = sb("wi", [128, NB, 8])
        mixA = sb("mixA", [64, 8, T], BF16); mixR = sb("mixR", [128, 4, T], BF16)
        BT = sb("BT", [128, 16, 128], BF16)
        cfar = sb("cfar", [128, 8])
        intra = sb("intra", [128, 512]); cdt = sb("cdt", [128, 512])
        kvd = sb("kvd", [128, 512]); kvdp = sb("kvdp", [128, 512]); pbias = sb("pbias", [128, 1])
        S = sb("S", [128, 512]); Sb = sb("Sb", [128, 512], BF16)
        stat = sb("stat", [128, NB, 2, 6]); mv = sb("mv", [128, NB, 2]); rstd = sb("rstd", [128, NB, 2])
        gst = sb("gst", [128, 4, 6]); gmv = sb("gmv", [128, 4, 2]); grs = sb("grs", [128, 4, 2])
        bs = sb("bs", [128, 8])

        for dst, src in ((ident, ident_d), (tri, tri_d), (jmat, jmat_d), (intra, intra_d), (cdt, cdt_d),
                         (kvd, kvd_d), (kvdp, kvdp_d), (pbias, pbias_d), (rbt, relb), (ohp, ohp_d)):
            P.dma("sp", dst[:], src)
        P.dma("sp", gng[:], gng_d.partition_broadcast(128))
        P.dma("sp", cfar[:], relb[31].partition_broadcast(128))
        P.copy("dve", identb[:], ident[:])
        P.memset("pool", onesf[:], 1.0)
        P.memset("pool", S[:], 0.0)
        P.memset("pool", Sb[:], 0.0)
        P.memset("pool", vflat[:, :], 0.0)
        P.memset("pool", vaug[:, :, :, 64:65], 1.0)

        P.mm(ps[0][0:8, 0:384], rbt[:, :], ohp[:, :])
        P.copy("dve", Fs[:], ps[0][0:8, 0:384])
        P.dma("sp", fscr, Fs[:])
        for hd in range(8):
            for v in range(2):
                n = hd * 2 + v
                P.add("sp", (lambda e, n=n, hd=hd, v=v: e.dma_start(
                    hk[n % 2][:], bass.AP(fscr.tensor, hd * 384 + v * 128, [[1, 128], [1, 128]]))),
                    rkeys=[(fscr.tensor.name, 0)], writes=[hk[n % 2][:]], dma=True)
                P.mm(ps[1 + n % 2][:, 0:128], jmat[:], hk[n % 2][:])
                P.copy("dve", BT[:, n, :], ps[1 + n % 2][:, 0:128])

        cast_rr = [0]
        jobs = []

        def prep_gu(w, sdst):
            for kc in range(8):
                jobs.append((w[kc * 128:(kc + 1) * 128, :], 128, DFF,
                             [(lambda sbv, kc=kc, sdst=sdst: (sdst[:, :, kc, :].rearrange("c p f -> p c f"),
                                                              sbv[:, 0:DFF].rearrange("p (c f) -> p c f", f=128)))]))

        def prep_d(w, sdst):
            for c in range(NFC):
                jobs.append((w[c * 128:(c + 1) * 128, :], 128, D,
                             [(lambda sbv, c=c, sdst=sdst: (sdst[c], sbv[:, 0:D]))]))

        prep_gu(w1g, s1g)
        prep_gu(w1u, s1u)
        prep_d(w1d, s1d)
        for kc in range(8):
            outs = []
            for ci, (c0, w) in enumerate(zip(FM_CHUNKS, FM_W)):
                outs.append(lambda sbv, ci=ci, c0=c0, w=w, kc=kc: (sfm[ci, :, kc, 0:w], sbv[:, c0:c0 + w]))
            outs.append(lambda sbv, kc=kc: (sfm[12, :, kc, 64:128], sbv[:, C_KI:C_KI + 64]))
            for gi, (c0, w) in enumerate(TM_GROUPS):
                outs.append(lambda sbv, gi=gi, c0=c0, w=w, kc=kc: (stm[gi, :, kc, 0:w], sbv[:, c0:c0 + w]))
            jobs.append((win[kc * 128:(kc + 1) * 128, :], 128, NIN, outs))
        for hd in range(8):
            jobs.append((wout[hd * 64:(hd + 1) * 64, :], 64, D, [(lambda sbv, hd=hd: (swoA[:, hd, :], sbv[0:64, 0:D]))]))
        for h in range(4):
            jobs.append((wout[512 + h * 128:512 + (h + 1) * 128, :], 128, D,
                         [(lambda sbv, h=h: (swoR[:, h, :], sbv[:, 0:D]))]))
        prep_gu(w2g, s2g)
        prep_gu(w2u, s2u)
        prep_d(w2d, s2d)

        def job_in(k):
            src, parts, n, _ = jobs[k]
            P.dma("sp", stages[k % 2][0:parts, 0:n], src)

        job_in(0)
        for k, (src, parts, n, outs) in enumerate(jobs):
            if k + 1 < len(jobs):
                job_in(k + 1)
            e = ("dve", "act", "pool")[k % 3]
            P.copy(e, stagebs[k % 2][0:parts, 0:n], stages[k % 2][0:parts, 0:n])
            for fn in outs:
                dst, srcv = fn(stagebs[k % 2])
                P.dma("sp", dst, srcv, waw=False)

        cur = {"nb": NB}

        def load_x(xsrc):
            nb = cur["nb"]
            P.dma("sp", xtm[:, 0:nb, :], xsrc.rearrange("(b p) d -> p b d", p=128))

        def transpose_x():
            nb = cur["nb"]; nt = nb * 128
            for kc in range(8):
                bank = ps[kc % 4]
                for b in range(nb):
                    P.tr(bank[:, b * 128:(b + 1) * 128], xtm[:, b, kc * 128:(kc + 1) * 128], ident[:])
                if kc % 2 == 0:
                    P.copy("dve", xT[:, kc, 0:nt], bank[:, 0:nt])
                else:
                    P.act(xT[:, kc, 0:nt], bank[:, 0:nt], AF.Copy)

        def ln_blocks(lng, lnb):
            P.dma("sp", GB[:, 0, :], lng.partition_broadcast(128))
            P.dma("sp", GB[:, 1, :], lnb.partition_broadcast(128))
            for b in range(cur["nb"]):
                for h in range(2):
                    P.add("dve", (lambda e, b=b, h=h: e.bn_stats(stat[:, b, h, :], xtm[:, b, h * 512:(h + 1) * 512])),
                          reads=[xtm[:, b, h * 512:(h + 1) * 512]], writes=[stat[:, b, h, :]])
                P.add("dve", (lambda e, b=b: e.bn_aggr(mv[:, b, :], stat[:, b, :, :])),
                      reads=[stat[:, b, :, :]], writes=[mv[:, b, :]])
                P.ts("dve", rstd[:, b, 0:1], mv[:, b, 1:2], EPS, None, ALU.add)
                P.act(rstd[:, b, 0:1], rstd[:, b, 0:1], AF.Sqrt)
                P.add("dve", (lambda e, b=b: e.reciprocal(rstd[:, b, 0:1], rstd[:, b, 0:1])),
                      reads=[rstd[:, b, 0:1]], writes=[rstd[:, b, 0:1]])
                P.stt("dve", rstd[:, b, 1:2], mv[:, b, 0:1], -1.0, rstd[:, b, 0:1], ALU.mult, ALU.mult)
                P.act(xtm[:, b, :], xtm[:, b, :], AF.Identity, bias=rstd[:, b, 1:2], scale=rstd[:, b, 0:1])
                P.tt("pool", xtm[:, b, :], xtm[:, b, :], GB[:, 0, :], ALU.mult)
                P.tt("pool", xtm[:, b, :], xtm[:, b, :], GB[:, 1, :], ALU.add)

        def ffn_core(sg_, su_, sd_, lng, lnb):
            nb = cur["nb"]; nt = nb * 128
            for c in range(NFC):
                s = c % 3
                P.dma("sp", wgb[s][:], sg_[c])
                P.dma("sp", wub[s][:], su_[c])
                pg = ps[(c % 2) * 2]
                pu = ps[(c % 2) * 2 + 1]
                for kc in range(8):
                    P.mm(pg[:, 0:nt], wgb[s][:, kc, :], xT[:, kc, 0:nt], start=(kc == 0), stop=(kc == 7))
                for kc in range(8):
                    P.mm(pu[:, 0:nt], wub[s][:, kc, :], xT[:, kc, 0:nt], start=(kc == 0), stop=(kc == 7))
                P.act(sg[c % 2][:, 0:nt], pg[:, 0:nt], AF.Silu)
                P.stt("dve", actb[:, c, 0:nt], pu[:, 0:nt], 0.5, sg[c % 2][:, 0:nt], ALU.mult, ALU.mult)
            for c in range(NFC):
                s = c % 3
                P.dma("sp", wdr[s][:], sd_[c])
                for b in range(nb):
                    for h in range(2):
                        P.mm(ps[b * 2 + h][:, :], actb[:, c, b * 128:(b + 1) * 128],
                             wdr[s][:, h * 512:(h + 1) * 512], start=(c == 0), stop=(c == NFC - 1))
            for b in range(nb):
                for h in range(2):
                    P.stt("dve", xtm[:, b, h * 512:(h + 1) * 512], xtm[:, b, h * 512:(h + 1) * 512], ALPHA,
                          ps[b * 2 + h][:, :], ALU.mult, ALU.add)
            ln_blocks(lng, lnb)

        def win_fm(tok0, own):
            chunks = range(13) if own else [4, 5, 6, 7, 12]
            for n, ci in enumerate(chunks):
                s = n % 3
                P.dma("sp", fmw[s][:], sfm[ci])
                bank = ps[n % 4]
                for kc in range(8):
                    P.mm(bank[:, :], fmw[s][:, kc, :], hT[:, kc, :], start=(kc == 0), stop=(kc == 7))
                if ci < 4:
                    P.act(qT[:, ci, :], bank[:, :], AF.Copy, scale=0.125)
                elif ci < 8:
                    P.copy("dve", kT[:, ci - 4, tok0:tok0 + T], bank[:, :])
                elif ci < 12:
                    P.act(qiT[:, ci - 8, :], bank[:, :], AF.Copy)
                else:
                    P.copy("dve", kiT2[:, tok0:tok0 + T], bank[:, :])

        def rotary(out, use_k, rp):
            x3 = rx[:, :].rearrange("p (h f) -> p h f", h=4)
            o3 = out.rearrange("p (h f) -> p h f", h=4)
            co = rp[:, 2 if use_k else 0, :].unsqueeze(1).broadcast_to([128, 4, 64])
            si = rp[:, 3 if use_k else 1, :].unsqueeze(1).broadcast_to([128, 4, 64])
            t3 = [tt_[:, :].rearrange("p (h f) -> p h f", h=4) for tt_ in tmp]
            x1 = x3[:, :, 0:64]
            x2 = x3[:, :, 64:128]
            P.tt("dve", t3[0], x1, co, ALU.mult)
            P.tt("pool", t3[1], x2, si, ALU.mult)
            P.tt("pool", t3[2], x1, si, ALU.mult)
            P.tt("dve", t3[3], x2, co, ALU.mult)
            P.tt("dve", o3[:, :, 0:64], t3[0], t3[1], ALU.subtract)
            P.tt("pool", o3[:, :, 64:128], t3[2], t3[3], ALU.add)

        def win_tm(row0, gblk0, own, rope_src):
            groups = list(globals().get("GROUPS", [0, 1, 2, 3, 4, 5, 6])) if own else [1, 4, 5]
            n = 0
            for gi in groups:
                c0, w = TM_GROUPS[gi]
                P.dma("sp", tmw[:, :, 0:w], stm[gi][:, :, 0:w])
                for b in range(NB):
                    bank = ps[4 + (n % 4)]
                    n += 1
                    for kc in range(8):
                        P.mm(bank[:, 0:w], hT[:, kc, b * 128:(b + 1) * 128], tmw[:, kc, 0:w],
                             start=(kc == 0), stop=(kc == 7))
                    st = ost[n % 3]
                    r0 = row0 + b * 128
                    if gi == 0:
                        P.act(st[:, :], bank[:, :], AF.Copy)
                        P.dma(OUTQ, o_k[r0:r0 + 128, :], st[:, :])
                    elif gi == 1:
                        if own:
                            P.copy("dve", st[:, :], bank[:, :])
                            P.dma(OUTQ, o_v[r0:r0 + 128, :], st[:, :])
                        P.act(vaug[:, gblk0 + b, :, 0:64], bank[:, :].rearrange("p (h d) -> p h d", h=8), AF.Copy)
                    elif gi == 2:
                        P.copy("dve", st[:, 0:64], bank[:, 0:64])
                        P.dma(OUTQ, o_ki[r0:r0 + 128, :], st[:, 0:64])
                        P.ts("dve", wi[:, b, :], bank[:, 64:72], WSCALE, None, ALU.mult)
                    elif gi in (3, 4):
                        rp = ropes[(n) % 2]
                        P.dma("sp", rp[:], rope_src[r0:r0 + 128])
                        P.act(rx[:, :], bank[:, :], AF.Copy)
                        rotary((qrot if gi == 3 else krot)[:, b, :], gi == 4, rp)
                    elif gi == 5:
                        P.act(vrb[:, b, :], bank[:, :], AF.Copy)
                    else:
                        P.act(sgr[:, b, :], bank[:, :], AF.Silu)

        tb = ps[3][:, :].bitcast(BF16)

        SUB = int(globals().get("SUB", 9))

        def ret_block(b, tcol0, own):
            hc = lambda h: slice(h * 128, (h + 1) * 128)
            if own and SUB < 1:
                return
            if own:
                for h in range(4):
                    P.tr(tb[:, hc(h)], qrot[:, b, hc(h)], identb[:])
                for h in range(4):
                    P.tr(tb[:, 512 + h * 128:512 + (h + 1) * 128], krot[:, b, hc(h)], identb[:])
                P.act(qkT[:, :], tb[:, :], AF.Copy)
                P.tt("pool", qpT[:, :], qkT[:, 0:512], cdt[:, :], ALU.mult)
                for h in range(4):
                    P.mm(ps[0][:, hc(h)], qkT[:, 512 + h * 128:512 + (h + 1) * 128], qkT[:, hc(h)])
                P.tt("dve", attm[:, :], ps[0][:, :], intra[:, :], ALU.mult)
                for h in range(4):
                    P.mm(ps[1][:, hc(h)], attm[:, hc(h)], vrb[:, b, hc(h)], start=True, stop=False)
                    P.mm(ps[1][:, hc(h)], qpT[:, hc(h)], Sb[:, hc(h)], start=False, stop=True)
            P.tt("pool", kd[:, :], krot[:, b, :], (kvd if own else kvdp)[:, :], ALU.mult)
            for h in range(4):
                P.mm(ps[2][:, hc(h)], kd[:, hc(h)], vrb[:, b, hc(h)])
            for h in range(4):
                P.stt("dve", S[:, hc(h)], S[:, hc(h)], CDK[h], ps[2][:, hc(h)], ALU.mult, ALU.add)
            P.act(Sb[:, :], S[:, :], AF.Copy)
            if not own or SUB < 2:
                return
            ret_tail(b, tcol0)

        def ret_tail(b, tcol0):
            hc = lambda h: slice(h * 128, (h + 1) * 128)
            for h in range(4):
                P.add("dve", (lambda e, h=h: e.bn_stats(gst[:, h, :], ps[1][:, hc(h)])),
                      reads=[ps[1][:, hc(h)]], writes=[gst[:, h, :]])
                P.add("dve", (lambda e, h=h: e.bn_aggr(gmv[:, h, :], gst[:, h, :])),
                      reads=[gst[:, h, :]], writes=[gmv[:, h, :]])
            P.ts("dve", grs[:, :, 0], gmv[:, :, 1], EPS, None, ALU.add)
            P.act(grs[:, :, 0], grs[:, :, 0], AF.Sqrt)
            P.add("dve", (lambda e: e.reciprocal(grs[:, :, 0], grs[:, :, 0])),
                  reads=[grs[:, :, 0]], writes=[grs[:, :, 0]])
            P.stt("dve", grs[:, :, 1], gmv[:, :, 0], -1.0, grs[:, :, 0], ALU.mult, ALU.mult)
            for h in range(4):
                P.act(yn[:, hc(h)], ps[1][:, hc(h)], AF.Identity, bias=grs[:, h, 1:2], scale=grs[:, h, 0:1])
            P.tt("pool", yn[:, :], yn[:, :], gng[:, :], ALU.mult)
            P.tt("pool", ytm[:, :], yn[:, :], sgr[:, b, :], ALU.mult)
            if SUB < 3:
                return
            for h in range(4):
                P.tr(tb[:, hc(h)], ytm[:, hc(h)], identb[:])
            P.copy("dve", mixR[:, :, tcol0:tcol0 + 128], tb[:, 0:512].rearrange("p (h t) -> p h t", h=4))

        def indexer(t, i):
            qb = 16 + t * NB + i
            W = (qb + 1) * 128
            for h in range(8):
                P.ts("dve", Dg[:, h, :], identb[:], wi[:, i, h:h + 1], None, ALU.mult)
            nch = (W + 511) // 512
            for c in range(nch):
                k0 = c * 512
                w = min(512, W - k0)
                accb = ps[3 + (c % 2)]
                def dots(h):
                    par, ch = h % 2, h // 2
                    P.mm(ps[h % 3][:, 0:w], qiT[par * 64:(par + 1) * 64, ch, i * 128:(i + 1) * 128],
                         kiT2[par * 64:(par + 1) * 64, k0:k0 + w])
                dots(0)
                dots(1)
                for h in range(8):
                    if h + 2 < 8:
                        dots(h + 2)
                    P.act(Rr[h % 3][:, 0:w], ps[h % 3][:, 0:w], AF.Relu)
                    P.mm(accb[:, 0:w], Dg[:, h, :], Rr[h % 3][:, 0:w], start=(h == 0), stop=(h == 7))
                last = (c == nch - 1)
                wf = w - 128 if last else w
                if wf > 0:
                    if k0 < HALF:
                        P.ts("dve", sc[:, k0:k0 + wf], accb[:, 0:wf], pbias[:, 0:1], None, ALU.add)
                    else:
                        P.act(sc[:, k0:k0 + wf], accb[:, 0:wf], AF.Copy)
                if last:
                    P.tt("dve", sc[:, W - 128:W], accb[:, w - 128:w], tri[:, :], ALU.add)
            mx, lo, mid, cnt, geh, flag, thr = [bs[:, k:k + 1] for k in range(7)]
            P.add("dve", (lambda e: e.reduce_max(mx, sc[:, 0:W], AX.X)), reads=[sc[:, 0:W]], writes=[mx])
            P.ts("dve", mid, mx, -R0 + R0 / 2.0, None, ALU.add)
            for k in range(NIT):
                half = R0 / (2.0 ** (k + 1))
                P.ts("dve", MB[i][:, 0:W], sc[:, 0:W], mid, None, ALU.is_ge, op1=ALU.add, accum_out=cnt)
                P.ts("dve", geh, cnt, 255.5, half, ALU.is_ge, ALU.mult)
                P.stt("dve", mid, geh, -half / 2.0, mid, ALU.add, ALU.add)
            hK = R0 / (2.0 ** (NIT + 1))
            P.ts("dve", MB[i][:, 0:W], sc[:, 0:W], -1.0e29, None, ALU.is_ge, op1=ALU.add, accum_out=cnt)
            P.ts("dve", flag, cnt, 255.5, 1.0e29, ALU.is_lt, ALU.mult)
            P.stt("dve", thr, mid, -hK, flag, ALU.add, ALU.subtract)
            P.ts("dve", MB[i][:, 0:W], sc[:, 0:W], thr, None, ALU.is_ge)
            if debug:
                P.dma(OUTQ, d_thr[t * NB + i], bs[:, 0:7])
                if t == NT - 1 and i == NB - 1:
                    P.dma(OUTQ, d_sc, sc[:, 0:SEQ])

        def attention(t):
            qb0 = 16 + t * NB
            jmax = qb0 + NB - 1
            MT = [Rr[0], Rr[1]]
            cnt = [0]
            for hh in range(2):
                its = []
                for j in range(jmax + 1):
                    for h4 in range(4):
                        its.append((j, h4, cnt[0]))
                        cnt[0] += 1

                def emit_mt(j):
                    imin = max(0, j - qb0)
                    c0 = imin * 128
                    half = (j % 2) * 512
                    for i in range(imin, NB):
                        P.tr(tb[:, half + i * 128:half + (i + 1) * 128], MB[i][:, j * 128:(j + 1) * 128], identb[:])
                    P.copy("dve", MT[j % 2][:, c0:T], tb[:, half + c0:half + T])

                def emit_logits(it):
                    j, h4, n = it
                    hd = hh * 4 + h4
                    par, ch = hd % 2, hd // 2
                    imin = max(0, j - qb0)
                    c0 = imin * 128
                    L = ps[n % 3]
                    mms = [(L[:, c0:T], kT[par * 64:(par + 1) * 64, ch, j * 128:(j + 1) * 128],
                            qT[par * 64:(par + 1) * 64, ch, c0:T])]
                    for i in range(imin, NB):
                        dd = qb0 + i - j
                        if dd in (0, 1):
                            mms.append((L[:, i * 128:(i + 1) * 128], identb[:, :], BT[:, hd * 2 + dd, :]))
                    for m, (o_, l_, r_) in enumerate(mms):
                        P.mm(o_, l_, r_, start=(m == 0), stop=(m == len(mms) - 1))

                def emit_pv(it):
                    j, h4, n = it
                    hd = hh * 4 + h4
                    imin = max(0, j - qb0)
                    c0 = imin * 128
                    ncol = T - c0
                    L = ps[n % 3]
                    pt = pT[n % 4]
                    P.act(pt[:, 0:ncol], L[:, c0:T], AF.Exp, bias=cfar[:, hd:hd + 1])
                    P.tt("dve", pt[:, 0:ncol], pt[:, 0:ncol], MT[j % 2][:, c0:T], ALU.mult)
                    v0 = (j * 8 + hd) * 65
                    P.mm(ps[4 + h4][:, c0:T], vflat[:, v0:v0 + 128], pt[:, 0:ncol],
                         start=(j == 0), stop=(j == jmax))

                emit_mt(0)
                emit_logits(its[0])
                for k, it in enumerate(its):
                    j, h4, n = it
                    if h4 == 1 and j + 1 <= jmax:
                        emit_mt(j + 1)
                    if k + 1 < len(its):
                        emit_logits(its[k + 1])
                    emit_pv(it)
                for h4 in range(4):
                    hd = hh * 4 + h4
                    P.act(osb[0:65, :], ps[4 + h4][0:65, :], AF.Copy)
                    P.add("dve", (lambda e: e.reciprocal(rden[64:65, :], osb[64:65, :])),
                          reads=[osb[64:65, :]], writes=[rden[64:65, :]])
                    P.mm(ps[3][0:64, :], onesf[64:65, 0:64], rden[64:65, :])
                    P.tt("dve", mixA[:, hd, :], osb[0:64, :], ps[3][0:64, :], ALU.mult)

        def wout_ln2():
            P.dma("sp", woA[:, :, :], swoA)
            P.dma("sp", woR[:, :, :], swoR)
            for b in range(cur["nb"]):
                bc = slice(b * 128, (b + 1) * 128)
                for h2 in range(2):
                    acc = ps[b * 2 + h2]
                    hcs = slice(h2 * 512, (h2 + 1) * 512)
                    for hd in range(8):
                        P.mm(acc[:, :], mixA[:, hd, bc], woA[:, hd, hcs], start=(hd == 0), stop=False)
                    for h in range(4):
                        P.mm(acc[:, :], mixR[:, h, bc], woR[:, h, hcs], start=False, stop=(h == 3))
                    P.stt("dve", xtm[:, b, hcs], xtm[:, b, hcs], ALPHA, acc[:, :], ALU.mult, ALU.add)
            ln_blocks(ln2g, ln2b)


        GAM = [math.exp(LG[h]) for h in range(4)]

        def sample_path():
            hc = lambda h: slice(h * 128, (h + 1) * 128)
            cur["nb"] = 1
            qm = carve(0, 4096, BF16).rearrange("p (h r t) -> p h r t", h=4, r=4)
            S0 = carve(4096, 2048); Sn = carve(6144, 2048)
            Snb = [carve(8192 + 1024 * r, 1024, BF16) for r in range(4)]
            kdr = carve(12288, 1024, BF16)
            kin = carve(13312, 256, BF16, parts=64)
            kas = carve(22528, 2048); vas = carve(24576, 2048)
            qis = carve(26624, 2048, BF16, parts=64).rearrange("p (h t) -> p h t", h=8)
            KPqs = [carve(0, 8192, F32, parts=64), carve(8192, 8192, F32, parts=64)]
            Ksel = carve(8192, 4096).rearrange("p (a f) -> p a f", a=2)
            Vsel = carve(12288, 4096).rearrange("p (a f) -> p a f", a=2)
            ltri_t = carve(16640, 512); iq = carve(17152, 1024); rowtab = carve(18176, 260); lstab = carve(18688, 260)
            tbt = carve(19200, 128)
            kiTs = carve(28672, 16640, BF16, parts=64)
            selrow = [carve(45312 + 512 * r, 512) for r in range(4)]
            sm = carve(47360, 6144)
            smi = sm.bitcast(I32)
            sc_all = carve(53504, 1040).rearrange("p (r c) -> p r c", r=4)
            m_all = carve(54592, 1040).rearrange("p (r c) -> p r c", r=4)
            OHs = [carve(55680, 1024), carve(60928, 1024)]
            OH = OHs[0]
            Rs = carve(56704, 2080); tmpS = carve(58816, 2080)
            tmpK = carve(56704, 2048)
            v = lambda a, n=1: sm[:, a:a + n]
            ptf, pt128, base, rowtot = v(0), v(1), v(2), v(3)
            lo4, mid4, cnt4, geh4, pm4 = v(8, 4), v(12, 4), v(16, 4), v(20, 4), v(24, 4)
            csA, csB, rank, phys = v(32, 65), v(97, 65), v(162, 65), v(227, 65)
            vals = v(292, 130).rearrange("p (c k) -> p c k", k=2)
            sel = v(422, 4).rearrange("p (a k) -> p a k", k=2)
            flag, nn = v(426, 2), v(428, 2)
            lg = v(430, 16).rearrange("p (a h) -> p a h", a=2)
            ee = v(446, 16).rearrange("p (a h) -> p a h", a=2)
            biasv, steps, wbs = v(462, 8), v(470, 32), v(502, 8)
            rdv = sm[0:64, 510:518]
            mcol = v(518)
            gm = sm[0:4, 521:522]; dgm = sm[0:4, 522:526]
            idx32 = smi[:, 526:528]; pt32 = smi[:, 528:529]
            tmpb = v(530, 256).rearrange("p (h b) -> p h b", h=8)
            drb = v(786, 256).rearrange("p (b h) -> p b h", h=8)
            RBb = v(1042, 256).rearrange("p (b h) -> p b h", h=8)

            load_x(xs_d)
            transpose_x()
            ffn_core(s1g, s1u, s1d, ln1g, ln1b)
            transpose_x()

            def tm_group(gi, bank):
                c0, w = TM_GROUPS[gi]
                P.dma("sp", tmw[:, :, 0:w], stm[gi][:, :, 0:w])
                for kc in range(8):
                    P.mm(bank[:, 0:w], hT[:, kc, 0:128], tmw[:, kc, 0:w], start=(kc == 0), stop=(kc == 7))

            bank = ps[4]
            tm_group(0, bank)
            P.act(kas[:, :], bank[:, :], AF.Copy)
            P.dma(OUTQ, o_ks, kas[0:4, :])
            bank = ps[5]
            tm_group(1, bank)
            P.act(vas[:, :], bank[:, :], AF.Copy)
            P.dma(OUTQ, o_vs, vas[0:4, :])
            bank = ps[6]
            tm_group(2, bank)
            P.copy("dve", rx[:, 0:64], bank[:, 0:64])
            P.dma(OUTQ, o_kis, rx[0:4, 0:64])
            P.ts("dve", wi[:, 0, :], bank[:, 64:72], WSCALE, None, ALU.mult)
            for n, ci in enumerate((0, 1, 2, 3)):
                sl = n % 3
                P.dma("sp", fmw[sl][:], sfm[ci])
                bank = ps[n % 4]
                for kc in range(8):
                    P.mm(bank[:, 0:128], fmw[sl][:, kc, :], hT[:, kc, 0:128], start=(kc == 0), stop=(kc == 7))
                P.act(qT[:, ci, 0:128], bank[:, 0:128], AF.Copy, scale=0.125)
            for n, ci in enumerate((8, 9, 10, 11, 12)):
                sl = (n + 1) % 3
                P.dma("sp", fmw[sl][:], sfm[ci])
                bank = ps[n % 4]
                for hf in range(2):
                    for kc in range(8):
                        P.mm(bank[0:64, hf * 128:(hf + 1) * 128], fmw[sl][:, kc, hf * 64:(hf + 1) * 64], hT[:, kc, 0:128],
                             start=(kc == 0), stop=(kc == 7))
                if ci < 12:
                    P.act(qis[:, 2 * (ci - 8):2 * (ci - 8) + 2, :], bank[0:64, 0:256].rearrange("p (a t) -> p a t", a=2), AF.Copy)
                else:
                    P.act(kin[:, :], bank[0:64, 0:128], AF.Copy)

            P.dma("sp", ropes[0][:], rope_s)
            for n, gi in enumerate((3, 4, 5, 6)):
                bank = ps[4 + n]
                tm_group(gi, bank)
                if gi in (3, 4):
                    P.act(rx[:, :], bank[:, :], AF.Copy)
                    rotary((qrot if gi == 3 else krot)[:, 0, :], gi == 4, ropes[0])
                elif gi == 5:
                    P.act(vrb[:, 0, :], bank[:, :], AF.Copy)
                else:
                    P.act(sgr[:, 0, :], bank[:, :], AF.Silu)
            for h in range(4):
                P.tr(tb[:, hc(h)], qrot[:, 0, hc(h)], identb[:])
            P.act(qkT[:, 0:512], tb[:, 0:512], AF.Copy)
            P.memset("pool", qm[:, :, :, :], 0.0)
            q3 = qkT[:, 0:512].rearrange("p (h t) -> p h t", h=4)
            for r in range(4):
                P.copy("dve", qm[:, :, r, r:r + 1], q3[:, :, r:r + 1])
            for r in range(4):
                P.dma("sp", S0[:, :].rearrange("p (h v) -> p h v", h=4), state_d[r].rearrange("h k v -> k h v"))
                P.ts("pool", kdr[:, :], krot[:, 0, :], ident[:, r:r + 1], None, ALU.mult)
                for h in range(4):
                    P.mm(ps[2][:, hc(h)], kdr[:, hc(h)], vrb[:, 0, hc(h)])
                for h in range(4):
                    P.stt("dve", Sn[:, hc(h)], S0[:, hc(h)], GAM[h], ps[2][:, hc(h)], ALU.mult, ALU.add)
                P.dma(OUTQ, o_Ss[r], Sn[:, :])
                P.act(Snb[r][:, :], Sn[:, :], AF.Copy)
            for h in range(4):
                for r in range(4):
                    P.mm(ps[1][:, hc(h)], qm[:, h, r, :], Snb[r][:, hc(h)], start=(r == 0), stop=(r == 3))
            ret_tail(0, 0)

            for dst, src in ((ltri_t, ltri_d), (iq, iotaq_d), (rowtab, rowtab_d), (lstab, lstab_d), (tbt, tbt_d)):
                P.dma("sp", dst[:, :], src)
            P.dma("sp", RBb[:, :, :].rearrange("p b h -> p (b h)"), relb.rearrange("b h -> (b h)").partition_broadcast(128))
            P.copy("dve", drb[:, 0:1, :], RBb[:, 0:1, :])
            P.tt("dve", drb[:, 1:32, :], RBb[:, 1:32, :], RBb[:, 0:31, :], ALU.subtract)
            for r in range(4):
                P.copy("dve", selrow[r][:, :], ident[:, r:r + 1].broadcast_to([128, 128]))
            P.copy("dve", vals[:, :, 1], lstab[:, :])

            P.copy("dve", kiTs[:, 8192:8320], kin[:, :])
            for r in range(4):
                P.dma("sp", smi[0:64, 528:529], pt_d[r].rearrange("(p o) -> p o", o=1))
                ev = 0
                for qq in range(4):
                    KPq = KPqs[qq % 2]
                    qix = smi[0:64, 529 + (qq % 2):530 + (qq % 2)]
                    P.ts("dve", qix, smi[0:64, 528:529], 4, qq, ALU.mult, ALU.add)
                    P.add("pool", (lambda e, KPq=KPq, qix=qix: e.indirect_dma_start(
                        KPq[:, :], None, ckidx_d, bass.IndirectOffsetOnAxis(ap=qix, axis=0))),
                        reads=[qix], writes=[KPq[:, :]], dma=True)
                    for r8 in range(4):
                        bank = ps[ev % 4]
                        for k in range(8):
                            row = r8 * 8 + k
                            P.tr(bank[0:64, k * 64:(k + 1) * 64], KPq[:, row * 64:(row + 1) * 64], ident[0:64, 0:64])
                        col0 = (qq * 32 + r8 * 8) * 64
                        if ev % 2 == 0:
                            P.act(kiTs[:, col0:col0 + 512], bank[0:64, :], AF.Copy)
                        else:
                            P.copy("dve", kiTs[:, col0:col0 + 512], bank[0:64, :])
                        ev += 1
                for c in range(65):
                    if c < 64:
                        o_ = ps[4][:, c * 8:(c + 1) * 8]
                    else:
                        o_ = ps[5][:, 0:8]
                    P.mm(o_, kiTs[:, c * 128:(c + 1) * 128], qis[:, :, r])
                P.act(Rs[:, 0:512], ps[4][:, :], AF.Relu)
                P.act(Rs[:, 512:520], ps[5][:, 0:8], AF.Relu)
                P.mm(ps[6][:, 0:8], selrow[r][:, :], wi[:, 0, :])
                P.copy("dve", wbs, ps[6][:, 0:8])
                R3 = Rs[:, 0:520].rearrange("p (c h) -> p c h", h=8)
                T3 = tmpS[:, 0:520].rearrange("p (c h) -> p c h", h=8)
                P.tt("dve", T3, R3, wbs.unsqueeze(1).broadcast_to([128, 65, 8]), ALU.mult)
                P.add("dve", (lambda e, r=r, T3=T3: e.tensor_reduce(sc_all[:, r, :], T3, AX.X, ALU.add)),
                      reads=[T3], writes=[sc_all[:, r, :]])
                P.ts("dve", mcol, ident[:, r:r + 1], -1.0, 1.0e30, ALU.add, ALU.mult)
                P.tt("dve", sc_all[:, r, 64:65], sc_all[:, r, 64:65], mcol, ALU.add)

            for r in range(4):
                P.add("dve", (lambda e, r=r: e.reduce_max(pm4[:, r:r + 1], sc_all[:, r, :], AX.X)),
                      reads=[sc_all[:, r, :]], writes=[pm4[:, r:r + 1]])
            P.tr(ps[7][0:4, 0:128], pm4, ident[:, :])
            P.add("dve", (lambda e: e.reduce_max(gm, ps[7][0:4, 0:128], AX.X)), reads=[ps[7][0:4, 0:128]], writes=[gm])
            P.ts("dve", dgm, ident[0:4, 0:4], gm, None, ALU.mult)
            P.mm(ps[7][:, 128:132], onesf[0:4, :], dgm)
            P.ts("dve", mid4, ps[7][:, 128:132], -R0 + R0 / 2.0, None, ALU.add)
            for k in range(NIT):
                half = R0 / (2.0 ** (k + 1))
                for r in range(4):
                    P.ts("dve", tmpS[:, 0:65], sc_all[:, r, :], mid4[:, r:r + 1], None, ALU.is_ge, op1=ALU.add,
                         accum_out=cnt4[:, r:r + 1])
                P.mm(ps[7][:, 136:140], onesf[:, :], cnt4)
                P.ts("dve", geh4, ps[7][:, 136:140], 255.5, half, ALU.is_ge, ALU.mult)
                P.stt("dve", mid4, geh4, -half / 2.0, mid4, ALU.add, ALU.add)
            P.ts("dve", lo4, mid4, -R0 / (2.0 ** (NIT + 1)), None, ALU.add)
            for r in range(4):
                P.ts("dve", m_all[:, r, :], sc_all[:, r, :], lo4[:, r:r + 1], None, ALU.is_ge)

            P.memset("pool", mixA[:, :, 0:128], 0.0)
            for r in range(4):
                m = m_all[:, r, :]
                P.add("dve", (lambda e, m=m: e.reduce_sum(rowtot, m, AX.X)), reads=[m], writes=[rowtot])
                src = m
                bufs = [csA, csB]
                sh = 1
                nb_ = 0
                while sh < 65:
                    dst = bufs[nb_ % 2]
                    P.copy("dve", dst[:, 0:sh], src[:, 0:sh])
                    P.tt("dve", dst[:, sh:65], src[:, sh:65], src[:, 0:65 - sh], ALU.add)
                    src = dst
                    sh *= 2
                    nb_ += 1
                P.mm(ps[7][:, 144:145], ltri_t[:, :], rowtot)
                P.copy("dve", base, ps[7][:, 144:145])
                P.ts("dve", rank, src, base, None, ALU.add)
                P.dma("sp", smi[0:64, 528:529], pt_d[r].rearrange("(p o) -> p o", o=1))
                P.dma("sp", smi[64:128, 528:529], pt_d[r].rearrange("(p o) -> p o", o=1))
                P.copy("dve", ptf, pt32)
                P.ts("dve", pt128, ptf, 128.0, None, ALU.mult)
                P.ts("dve", phys, rowtab[:, :], pt128, None, ALU.add)
                P.memset("pool", phys[:, 64:65], 0.0)
                P.copy("dve", vals[:, :, 0], phys)
                for c in range(65):
                    OHc = OHs[c % 2]
                    P.ts("dve", OHc[:, :], iq[:, :], rank[:, c:c + 1], m[:, c:c + 1], ALU.is_equal, ALU.mult)
                    for a in range(2):
                        P.mm(ps[6 + a][:, 16:18], OHc[:, a * 128:(a + 1) * 128], vals[:, c, :],
                             start=(c == 0), stop=(c == 64))
                for a in range(2):
                    P.copy("dve", sel[:, a, :], ps[6 + a][:, 16:18])
                P.copy("dve", idx32, sel[:, :, 0])
                for a in range(2):
                    P.add("pool", (lambda e, a=a: e.indirect_dma_start(
                        Ksel[:, a, :], None, ck_d, bass.IndirectOffsetOnAxis(ap=idx32[:, a:a + 1], axis=0))),
                        reads=[idx32[:, a:a + 1]], writes=[Ksel[:, a, :]], dma=True)
                    P.add("pool", (lambda e, a=a: e.indirect_dma_start(
                        Vsel[:, a, :], None, cv_d, bass.IndirectOffsetOnAxis(ap=idx32[:, a:a + 1], axis=0))),
                        reads=[idx32[:, a:a + 1]], writes=[Vsel[:, a, :]], dma=True)
                P.ts("dve", flag, sel[:, :, 1], 8191.5, None, ALU.is_ge)
                P.ts("dve", nn, sel[:, :, 1], -1.0, 8192.0, ALU.mult, ALU.add)
                for (cache_sel, new_tm, pb) in ((Ksel, kas, ps[4]), (Vsel, vas, ps[5])):
                    P.mm(pb[:, :], selrow[r][:, :], new_tm[:, :])
                    for a in range(2):
                        P.tt("dve", tmpK[:, :], pb[:, :], cache_sel[:, a, :], ALU.subtract)
                        P.stt("dve", cache_sel[:, a, :], tmpK[:, :], flag[:, a:a + 1], cache_sel[:, a, :], ALU.mult, ALU.add)
                for ci in range(4):
                    P.copy("dve", OH[:, 0:64].bitcast(BF16), qT[:, ci, r:r + 1].broadcast_to([128, 128]))
                    P.mm(ps[3][:, ci * 128:(ci + 1) * 128], OH[:, 0:64].bitcast(BF16), identb[:, :])
                for a in range(2):
                    P.tt("dve", tmpK[:, :], Ksel[:, a, :], ps[3][:, :], ALU.mult)
                    P.add("dve", (lambda e, a=a: e.tensor_reduce(lg[:, a, :], tmpK[:, :].rearrange("p (h d) -> p h d", h=8),
                                                               AX.X, ALU.add)),
                          reads=[tmpK[:, :]], writes=[lg[:, a, :]])
                    P.ts("dve", steps, tbt[:, :], nn[:, a:a + 1], None, ALU.is_le)
                    P.tt("dve", tmpb, drb.rearrange("p b h -> p h b"), steps.unsqueeze(1).broadcast_to([128, 8, 32]), ALU.mult)
                    P.add("dve", (lambda e: e.tensor_reduce(biasv, tmpb, AX.X, ALU.add)), reads=[tmpb], writes=[biasv])
                    P.tt("dve", lg[:, a, :], lg[:, a, :], biasv, ALU.add)
                P.act(ee, lg, AF.Exp)
                for hd in range(8):
                    for a in range(2):
                        P.mm(ps[5][0:64, hd:hd + 1], Vsel[:, a, hd * 64:(hd + 1) * 64], ee[:, a, hd:hd + 1],
                             start=(a == 0), stop=(a == 1))
                for a in range(2):
                    P.mm(ps[6][0:64, 32:40], onesf[:, 0:64], ee[:, a, :], start=(a == 0), stop=(a == 1))
                P.add("dve", (lambda e: e.reciprocal(rdv, ps[6][0:64, 32:40])), reads=[ps[6][0:64, 32:40]], writes=[rdv])
                P.tt("dve", mixA[:, :, r], ps[5][0:64, 0:8], rdv, ALU.mult)

            wout_ln2()
            transpose_x()
            ffn_core(s2g, s2u, s2d, ln3g, ln3b)
            P.dma(OUTQ, o_ys, xtm[0:4, 0, :])
            cur["nb"] = NB

        STAGE = int(globals().get("STAGE", 6))
        TILES = int(globals().get("TILES", NT))
        for t in range(TILES if STAGE >= 1 else 0):
            load_x(xpre[t * T:(t + 1) * T, :])
            transpose_x()
            ffn_core(s1g, s1u, s1d, ln1g, ln1b)
            transpose_x()
            win_fm(t * T, False)
            win_tm(t * T, t * NB, False, rope_pre)
            for b in range(NB):
                ret_block(b, b * 128, False)
        for t in range(TILES if STAGE >= 2 else 0):
            load_x(xown[t * T:(t + 1) * T, :])
            transpose_x()
            ffn_core(s1g, s1u, s1d, ln1g, ln1b)
            transpose_x()
            win_fm(HALF + t * T, True)
            win_tm(t * T, 16 + t * NB, True, rope_own)
            for b in range(NB):
                ret_block(b, b * 128, True)
            if STAGE < 3:
                continue
            for i in range(NB):
                indexer(t, i)
            if STAGE < 4:
                continue
            attention(t)
            if debug:
                P.dma(OUTQ, d_mixA[t], mixA[:, :, :].rearrange("p h t -> p (h t)"))
                P.dma(OUTQ, d_mixR[t], mixR[:, :, :].rearrange("p h t -> p (h t)"))
            if STAGE < 5:
                continue
            wout_ln2()
            transpose_x()
            ffn_core(s2g, s2u, s2d, ln3g, ln3b)
            P.dma(OUTQ, o_y[t * T:(t + 1) * T, :].rearrange("(b p) d -> p b d", p=128), xtm[:, :, :])
        P.dma(OUTQ, o_S, S[:, :])
        if STAGE >= 6:
            sample_path()

        P.emit()
    return nc


def kernel(**inputs):
    x = np.asarray(inputs["x_prompt"], dtype=np.float32)
    nc = bass.Bass("TRN2", target_bir_lowering=False)
    dbg = bool(globals().get("DEBUG", False))
    build(nc, debug=dbg)
    shared = host_consts()
    for k in ("ffn1_wg", "ffn1_wu", "ffn1_wd", "ffn2_wg", "ffn2_wu", "ffn2_wd", "w_in", "w_out"):
        shared[k] = np.ascontiguousarray(inputs[k][0], dtype=np.float32)
    for k in ("ln1_g", "ln1_b", "ln2_g", "ln2_b", "ln3_g", "ln3_b"):
        shared[k] = np.ascontiguousarray(inputs[k], dtype=np.float32).reshape(D)
    shared["ret_gn_g"] = np.ascontiguousarray(inputs["ret_gn_g"], dtype=np.float32).reshape(512)
    shared["rel_bias"] = np.ascontiguousarray(inputs["rel_bias"], dtype=np.float32)
    shared["cache_kidx"] = np.ascontiguousarray(inputs["cache_kidx"][0], dtype=np.float32).reshape(2560 * 4, 2048)
    shared["cache_k"] = np.ascontiguousarray(inputs["cache_k"][0], dtype=np.float32).reshape(2560 * 128, 512)
    shared["cache_v"] = np.ascontiguousarray(inputs["cache_v"][0], dtype=np.float32).reshape(2560 * 128, 512)
    xsamp = np.asarray(inputs["x_sample"], dtype=np.float32).reshape(32, D)
    ptab = np.ascontiguousarray(inputs["page_table"], dtype=np.int32)
    sret = np.asarray(inputs["state_ret"], dtype=np.float32)[0]
    rope_lo = rope_table(np.arange(0, HALF))
    rope_hi = rope_table(np.arange(HALF, SEQ))
    in_maps = []
    for c in range(8):
        b, h = c // 2, c % 2
        m = dict(shared)
        m["xpre"] = np.ascontiguousarray(x[b, 0:HALF])
        m["xown"] = np.ascontiguousarray(x[b, h * HALF:(h + 1) * HALF])
        m["rope_pre"] = rope_lo
        m["rope_own"] = rope_hi if h else rope_lo
        m["kvd_pre"] = shared["kvd"] if h else np.zeros((128, 512), np.float32)
        m["pbias"] = np.full((128, 1), 0.0 if h else NEG, np.float32)
        xs = np.zeros((128, D), np.float32)
        xs[0:4] = xsamp[c * 4:(c + 1) * 4]
        m["xs"] = xs
        m["pt_s"] = np.ascontiguousarray(ptab[c * 4:(c + 1) * 4])
        m["state_s"] = np.ascontiguousarray(sret[c * 4:(c + 1) * 4])
        in_maps.append(m)
    res = run_bass_kernel_spmd(nc, in_maps, core_ids=list(range(8)))
    r = res.results
    if dbg:
        globals()["LAST"] = r
    y_p = np.zeros((4, SEQ, D), np.float32)
    k_p = np.zeros((1, 4, SEQ, 8, 64), np.float32)
    v_p = np.zeros((1, 4, SEQ, 8, 64), np.float32)
    ki_p = np.zeros((1, 4, SEQ, 64), np.float32)
    s_p = np.zeros((1, 4, 4, 128, 128), np.float32)
    for c in range(8):
        b, h = c // 2, c % 2
        sl = slice(h * HALF, (h + 1) * HALF)
        y_p[b, sl] = r[c]["o_y"]
        k_p[0, b, sl] = r[c]["o_k"].reshape(HALF, 8, 64)
        v_p[0, b, sl] = r[c]["o_v"].reshape(HALF, 8, 64)
        ki_p[0, b, sl] = r[c]["o_ki"].reshape(HALF, 64)
        if h == 1:
            s_p[0, b] = r[c]["o_S"].reshape(128, 4, 128).transpose(1, 0, 2)
    y_s = np.zeros((32, 1, D), np.float32)
    k_s = np.zeros((1, 32, 1, 8, 64), np.float32)
    v_s = np.zeros((1, 32, 1, 8, 64), np.float32)
    ki_s = np.zeros((1, 32, 1, 64), np.float32)
    s_s = np.zeros((1, 32, 4, 128, 128), np.float32)
    for c in range(8):
        sl = slice(c * 4, (c + 1) * 4)
        y_s[sl, 0] = r[c]["o_ys"]
        k_s[0, sl, 0] = r[c]["o_ks"].reshape(4, 8, 64)
        v_s[0, sl, 0] = r[c]["o_vs"].reshape(4, 8, 64)
        ki_s[0, sl, 0] = r[c]["o_kis"]
        s_s[0, sl] = r[c]["o_Ss"].reshape(4, 128, 4, 128).transpose(0, 2, 1, 3)
    return (y_p, y_s, k_p, v_p, ki_p, s_p, k_s, v_s, ki_s, s_s)
```

```python
import numpy as np
from contextlib import ExitStack
import concourse.bass as bass
import concourse.mybir as mybir
from concourse.bass_utils import run_bass_kernel_spmd

F32 = mybir.dt.float32
BF16 = mybir.dt.bfloat16
I32 = mybir.dt.int32
ALU = mybir.AluOpType
AF = mybir.ActivationFunctionType
AX = mybir.AxisListType
ESZ = {F32: 4, BF16: 2, I32: 4}

GRAN = 512
ENGS = ("pe", "act", "dve", "pool", "sp")
NDSEM = 8


class Op:
    __slots__ = ("eng", "fn", "deps", "is_dma", "sig", "signo", "dsem", "dval", "idx")


class Prog:
    def __init__(self, nc):
        self.nc = nc
        self.ops = {e: [] for e in ENGS}
        self.lastw = {}
        self.readers = {}
        self.dmas = {e: [] for e in ENGS}
        self.tracked_dram = set()

    def ap_keys(self, ap):
        name = ap.tensor.name
        sp = str(ap.space)
        if "DRAM" in sp.upper() or "HBM" in sp.upper():
            if name in self.tracked_dram:
                return [(name, 0)]
            return []
        if sp != "SB":
            return [(name, 0)]
        pat = ap.ap
        ps = pat[0][0]
        off = ap.offset % ps if ps > 0 else ap.offset
        ext = 1
        for st, cnt in pat[1:]:
            ext += (cnt - 1) * abs(st)
        es = ESZ[ap.dtype]
        lo = off * es
        hi = (off + ext) * es
        return [(name, g) for g in range(lo // GRAN, (hi - 1) // GRAN + 1)]

    def add(self, eng, fn, reads=(), writes=(), rkeys=(), wkeys=(), dma=False, waw=True):
        op = Op()
        op.eng = eng
        op.fn = fn
        op.is_dma = dma
        op.sig = False
        op.signo = 0
        op.idx = len(self.ops[eng])
        deps = {}
        rk = list(rkeys)
        for ap in reads:
            rk += self.ap_keys(ap)
        wk = list(wkeys)
        for ap in writes:
            wk += self.ap_keys(ap)

        def adddep(d):
            if d is op:
                return
            if d.is_dma:
                deps[("d", id(d))] = d
            else:
                k = ("c", d.eng)
                if k not in deps or deps[k].idx < d.idx:
                    deps[k] = d

        for k in rk:
            for d in self.lastw.get(k, ()):
                adddep(d)
            if k[0].startswith("ps"):
                r = self.readers.get(k)
                if r:
                    for e2, d in r[0].items():
                        if e2 != eng:
                            adddep(d)
        for k in wk:
            r = self.readers.get(k)
            if r:
                for d in r[0].values():
                    adddep(d)
                for d in r[1]:
                    adddep(d)
            if waw:
                for d in self.lastw.get(k, ()):
                    adddep(d)
        for k in rk:
            r = self.readers.get(k)
            if r is None:
                r = self.readers[k] = ({}, [])
            if dma:
                r[1].append(op)
            else:
                r[0][eng] = op
        for k in wk:
            if waw:
                self.lastw[k] = [op]
            else:
                self.lastw.setdefault(k, []).append(op)
            self.readers[k] = ({}, [])
        if dma:
            lst = self.dmas[eng]
            n = len(lst)
            op.dsem = n % NDSEM
            op.dval = 16 * (n // NDSEM + 1)
            if n >= NDSEM:
                adddep(lst[n - NDSEM])
            lst.append(op)
        op.deps = list(deps.values())
        self.ops[eng].append(op)
        return op

    def mm(self, out, lhsT, rhs, start=True, stop=True, **kw):
        return self.add("pe", lambda e: e.matmul(out, lhsT, rhs, start=start, stop=stop, **kw),
                        reads=[lhsT, rhs], writes=[out])

    def tr(self, out, in_, ident):
        return self.add("pe", lambda e: e.transpose(out, in_, ident), reads=[in_, ident], writes=[out])

    def act(self, out, in_, func, bias=None, scale=None, eng="act"):
        kw = {}
        rd = [in_]
        if bias is not None:
            kw["bias"] = bias
            if not isinstance(bias, (int, float)):
                rd.append(bias)
        if scale is not None:
            kw["scale"] = scale
            if not isinstance(scale, (int, float)):
                rd.append(scale)
        return self.add(eng, lambda e: e.activation(out, in_, func, **kw), reads=rd, writes=[out])

    def ts(self, eng, out, in0, s1, s2, op0, op1=None, accum_out=None):
        rd = [in0]
        for s in (s1, s2):
            if s is not None and not isinstance(s, (int, float)):
                rd.append(s)
        wr = [out]
        kw = {}
        if op1 is not None:
            kw["op1"] = op1
        if accum_out is not None:
            kw["accum_out"] = accum_out
            wr.append(accum_out)
        return self.add(eng, lambda e: e.tensor_scalar(out, in0, s1, s2, op0, **kw), reads=rd, writes=wr)

    def tt(self, eng, out, in0, in1, op):
        return self.add(eng, lambda e: e.tensor_tensor(out, in0, in1, op), reads=[in0, in1], writes=[out])

    def stt(self, eng, out, in0, scalar, in1, op0, op1):
        rd = [in0, in1]
        if not isinstance(scalar, (int, float)):
            rd.append(scalar)
        return self.add(eng, lambda e: e.scalar_tensor_tensor(out, in0, scalar, in1, op0, op1),
                        reads=rd, writes=[out])

    def copy(self, eng, out, in_):
        if eng == "act":
            return self.add(eng, lambda e: e.copy(out, in_), reads=[in_], writes=[out])
        return self.add(eng, lambda e: e.tensor_copy(out, in_), reads=[in_], writes=[out])

    def memset(self, eng, ap, val):
        return self.add(eng, lambda e: e.memset(ap, val), writes=[ap])

    def dma(self, q, out, in_, waw=True, **kw):
        return self.add(q, lambda e: e.dma_start(out, in_, **kw), reads=[in_], writes=[out], dma=True, waw=waw)

    def emit(self):
        nc = self.nc
        for e in ENGS:
            for op in self.ops[e]:
                for d in op.deps:
                    d.sig = True
        for e in ENGS:
            c = 0
            for op in self.ops[e]:
                if (not op.is_dma) and op.sig:
                    c += 1
                    op.signo = c
        with ExitStack() as es:
            csem = {e: es.enter_context(nc.semaphore("c_" + e)) for e in ENGS}
            dsem = {e: [es.enter_context(nc.semaphore("d_%s_%d" % (e, i))) for i in range(NDSEM)]
                    for e in ENGS if self.dmas[e]}
            block = es.enter_context(nc.Block())

            def run(e, eng):
                waited = {}
                for op in self.ops[e]:
                    need = {}
                    for d in op.deps:
                        if d.is_dma:
                            key = (d.eng, d.dsem)
                            val = d.dval
                        else:
                            if d.eng == "pe" and e == "pe":
                                continue
                            key = (d.eng, -1)
                            val = d.signo
                        if need.get(key, 0) < val:
                            need[key] = val
                    for key, val in need.items():
                        if waited.get(key, 0) < val:
                            sem = csem[key[0]] if key[1] < 0 else dsem[key[0]][key[1]]
                            eng.wait_ge(sem, val)
                            waited[key] = val
                    ins = op.fn(eng)
                    if op.is_dma:
                        ins.then_inc(dsem[e][op.dsem], 16)
                    elif op.sig:
                        ins.then_inc(csem[e], 1)
                lst = self.dmas[e]
                if lst:
                    fin = {}
                    for d in lst:
                        fin[d.dsem] = max(fin.get(d.dsem, 0), d.dval)
                    for i, v in fin.items():
                        if waited.get((e, i), 0) < v:
                            eng.wait_ge(dsem[e][i], v)

            @block.tensor
            def _(eng):
                run("pe", eng)

            @block.scalar
            def _(eng):
                run("act", eng)

            @block.vector
            def _(eng):
                run("dve", eng)

            @block.gpsimd
            def _(eng):
                run("pool", eng)

            @block.sync
            def _(eng):
                run("sp", eng)


import math

D = 1024
DFF = 2816
NFC = DFF // 128
NIN = 4168
T = 512
NB = T // 128
HALF = 2048
SEQ = 4096
NT = HALF // T
ALPHA = 2.0 ** 0.25
EPS = 1e-5
WSCALE = (64 ** -0.5) * (8 ** -0.5)
R0 = 16.0
NIT = 16
NEG = -1.0e30
MNEG = -30000.0
ARENA = 65536
OUTQ = "sp"

C_QA, C_KA, C_VA, C_QI, C_KI, C_WI, C_QR, C_KR, C_VR, C_GR = 0, 512, 1024, 1536, 2048, 2112, 2120, 2632, 3144, 3656
FM_CHUNKS = [C_QA + 128 * i for i in range(4)] + [C_KA + 128 * i for i in range(4)] + \
            [C_QI + 128 * i for i in range(4)] + [C_KI]
FM_W = [128] * 12 + [64]
TM_GROUPS = [(C_KA, 512), (C_VA, 512), (C_KI, 72), (C_QR, 512), (C_KR, 512), (C_VR, 512), (C_GR, 512)]

LG = [math.log1p(-2.0 ** (-5.0 - h)) for h in range(4)]
CDK = [math.exp(128.0 * LG[h]) for h in range(4)]


def host_consts():
    c = {}
    c["ident"] = np.eye(128, dtype=np.float32)
    t = np.arange(128)
    c["tri"] = np.where(t[None, :] <= t[:, None], 0.0, NEG).astype(np.float32)
    c["jmat"] = np.eye(128, dtype=np.float32)[::-1].copy()
    n = np.maximum(np.arange(384) - 127, 0)
    nf = np.maximum(n, 1).astype(np.float32)
    large = 16 + (np.log(nf / np.float32(16)) / np.float32(math.log(8.0)) * np.float32(16)).astype(np.int32)
    bk = np.where(n < 16, n, np.minimum(large, 31))
    oh = np.zeros((32, 384), np.float32)
    oh[bk, np.arange(384)] = 1.0
    oh[31, :] -= 1.0
    c["ohp"] = oh
    i = np.arange(128)
    intra = np.zeros((128, 4, 128), np.float64)
    cdt = np.zeros((128, 4, 128), np.float64)
    kvd = np.zeros((128, 4, 128), np.float64)
    for h in range(4):
        diff = i[None, :] - i[:, None]
        intra[:, h, :] = np.where(diff >= 0, np.exp(np.maximum(diff, 0) * LG[h]), 0.0)
        cdt[:, h, :] = np.exp((i[None, :] + 1.0) * LG[h])
        kvd[:, h, :] = np.exp((127.0 - i[:, None]) * LG[h])
    c["intra"] = intra.reshape(128, 512).astype(np.float32)
    c["cdt"] = cdt.reshape(128, 512).astype(np.float32)
    c["kvd"] = kvd.reshape(128, 512).astype(np.float32)
    p = np.arange(128)
    c["ltri"] = (p[:, None] < p[None, :]).astype(np.float32)
    c["iotaq"] = np.tile(np.arange(1, 257, dtype=np.float32)[None, :], (128, 1))
    cc = np.arange(64)
    rowtab = np.zeros((128, 65), np.float32)
    rowtab[:, :64] = 2 * cc[None, :] + (p[:, None] // 64)
    lstab = np.zeros((128, 65), np.float32)
    lstab[:, :64] = (p[:, None] % 64) * 128 + rowtab[:, :64]
    lstab[:, 64] = 8192.0
    c["rowtab"] = rowtab
    c["lstab"] = lstab
    nn = np.arange(0, 8193)
    nf = np.maximum(nn, 1).astype(np.float32)
    large = 16 + (np.log(nf / np.float32(16)) / np.float32(math.log(8.0)) * np.float32(16)).astype(np.int32)
    bk = np.where(nn < 16, nn, np.minimum(large, 31))
    tb = np.array([float(np.argmax(bk >= b)) for b in range(32)], np.float32)
    c["tbt"] = np.tile(tb[None, :], (128, 1))
    c["rope_s"] = rope_table(np.full(128, 8192))
    return c


def rope_table(pos):
    half = 64
    freqs = (np.float32(10000.0) ** (-np.arange(half, dtype=np.float32) / np.float32(half))).astype(np.float32)
    ang = pos.astype(np.float32)[:, None] * freqs[None, :]
    co = np.cos(ang).astype(np.float32)
    si = np.sin(ang).astype(np.float32)
    s = np.float32(128.0 ** -0.5)
    return np.stack([co, si, co * s, si * s], axis=1).astype(np.float32)


def build(nc, debug=False):
    def din(name, shape, dt=F32):
        return nc.dram_tensor(name, list(shape), dt, kind="ExternalInput").ap()

    def dout(name, shape, dt=F32):
        return nc.dram_tensor(name, list(shape), dt, kind="ExternalOutput").ap()

    xpre = din("xpre", [HALF, D])
    xown = din("xown", [HALF, D])
    ident_d = din("ident", [128, 128]); tri_d = din("tri", [128, 128]); jmat_d = din("jmat", [128, 128])
    ohp_d = din("ohp", [32, 384]); intra_d = din("intra", [128, 512]); cdt_d = din("cdt", [128, 512])
    kvd_d = din("kvd", [128, 512]); kvdp_d = din("kvd_pre", [128, 512]); pbias_d = din("pbias", [128, 1])
    rope_own = din("rope_own", [HALF, 4, 64]); rope_pre = din("rope_pre", [HALF, 4, 64])
    relb = din("rel_bias", [32, 8])
    w1g = din("ffn1_wg", [D, DFF]); w1u = din("ffn1_wu", [D, DFF]); w1d = din("ffn1_wd", [DFF, D])
    w2g = din("ffn2_wg", [D, DFF]); w2u = din("ffn2_wu", [D, DFF]); w2d = din("ffn2_wd", [DFF, D])
    win = din("w_in", [D, NIN]); wout = din("w_out", [D, D])
    ln1g = din("ln1_g", [D]); ln1b = din("ln1_b", [D])
    ln2g = din("ln2_g", [D]); ln2b = din("ln2_b", [D])
    ln3g = din("ln3_g", [D]); ln3b = din("ln3_b", [D])
    gng_d = din("ret_gn_g", [512])

    xs_d = din("xs", [128, D]); pt_d = din("pt_s", [4, 64], I32); state_d = din("state_s", [4, 4, 128, 128])
    ckidx_d = din("cache_kidx", [2560 * 4, 2048]); ck_d = din("cache_k", [2560 * 128, 512]); cv_d = din("cache_v", [2560 * 128, 512])
    rope_s = din("rope_s", [128, 4, 64]); ltri_d = din("ltri", [128, 128]); iotaq_d = din("iotaq", [128, 256])
    rowtab_d = din("rowtab", [128, 65]); lstab_d = din("lstab", [128, 65]); tbt_d = din("tbt", [128, 32])
    o_ys = dout("o_ys", [4, D]); o_ks = dout("o_ks", [4, 512]); o_vs = dout("o_vs", [4, 512]); o_kis = dout("o_kis", [4, 64])
    o_Ss = dout("o_Ss", [4, 128, 512])
    o_y = dout("o_y", [HALF, D])
    o_k = dout("o_k", [HALF, 512]); o_v = dout("o_v", [HALF, 512]); o_ki = dout("o_ki", [HALF, 64])
    o_S = dout("o_S", [128, 512])
    if debug:
        d_mixA = dout("d_mixA", [NT, 64, 8 * T], BF16)
        d_mixR = dout("d_mixR", [NT, 128, 4 * T], BF16)
        d_sc = dout("d_sc", [128, SEQ])
        d_thr = dout("d_thr", [NT * NB, 128, 7])

    def scr(name, shape, dt=BF16):
        return nc.dram_tensor(name, list(shape), dt, kind="Internal").ap()

    s1g = scr("s1g", [NFC, 128, 8, 128]); s1u = scr("s1u", [NFC, 128, 8, 128]); s1d = scr("s1d", [NFC, 128, D])
    s2g = scr("s2g", [NFC, 128, 8, 128]); s2u = scr("s2u", [NFC, 128, 8, 128]); s2d = scr("s2d", [NFC, 128, D])
    sfm = scr("sfm", [13, 128, 8, 128]); stm = scr("stm", [7, 128, 8, 512])
    swoA = scr("swoA", [64, 8, D]); swoR = scr("swoR", [128, 4, D])
    fscr = scr("fscr", [8, 384], F32)

    P = Prog(nc)
    for a in (s1g, s1u, s1d, s2g, s2u, s2d, sfm, stm, swoA, swoR, fscr):
        P.tracked_dram.add(a.tensor.name)

    with ExitStack() as es:
        def sb(name, shape, dt=F32):
            return es.enter_context(nc.sbuf_tensor("sb_" + name, list(shape), dt))

        ps = [es.enter_context(nc.psum_tensor("ps%d" % i, [128, 512], F32)) for i in range(8)]
        arena = sb("arena", [128, ARENA // 4])

        def carve(off, nbytes, dt=F32, parts=128):
            a = arena[0:parts, off // 4:(off + nbytes) // 4]
            return a.bitcast(dt) if dt != F32 else a

        xT = carve(0, 8192, BF16).rearrange("p (k t) -> p k t", k=8)
        actb = carve(8192, 22528, BF16).rearrange("p (c t) -> p c t", c=NFC)
        sg = [carve(30720 + 2048 * i, 2048) for i in range(2)]
        wgb = [carve(34816 + 2048 * i, 2048, BF16).rearrange("p (k f) -> p k f", k=8) for i in range(3)]
        wub = [carve(40960 + 2048 * i, 2048, BF16).rearrange("p (k f) -> p k f", k=8) for i in range(3)]
        wdr = [carve(47104 + 2048 * i, 2048, BF16) for i in range(3)]
        stages = [carve(0, 16672), carve(16896, 16672)]
        stagebs = [carve(33792, 8336, BF16), carve(42240, 8336, BF16)]
        hT = xT
        fmw = [carve(8192 + 2048 * i, 2048, BF16).rearrange("p (k f) -> p k f", k=8) for i in range(3)]
        tmw = carve(14336, 8192, BF16).rearrange("p (k f) -> p k f", k=8)
        ost = [carve(22528 + 2048 * i, 2048) for i in range(3)]
        qrot = carve(28672, 4096, BF16).rearrange("p (b f) -> p b f", b=NB)
        krot = carve(32768, 4096, BF16).rearrange("p (b f) -> p b f", b=NB)
        vrb = carve(36864, 4096, BF16).rearrange("p (b f) -> p b f", b=NB)
        sgr = carve(40960, 8192).rearrange("p (b f) -> p b f", b=NB)
        rx = carve(49152, 2048)
        tmp = [carve(51200 + 1024 * i, 1024) for i in range(4)]
        qkT = carve(55296, 2048, BF16)
        qpT = carve(57344, 1024, BF16)
        attm = carve(58368, 1024, BF16)
        kd = carve(59392, 1024, BF16)
        yn = carve(60416, 2048)
        ytm = carve(62464, 1024, BF16)
        ropes = [carve(63488 + 1024 * i, 1024).rearrange("p (a f) -> p a f", a=4) for i in range(2)]
        sc = carve(0, 16384)
        MB = [carve(16384 + 8192 * i, 8192, BF16) for i in range(4)]
        Rr = [carve(49152 + 1024 * i, 1024, BF16) for i in range(3)]
        Dg = carve(52224, 2048, BF16).rearrange("p (h t) -> p h t", h=8)
        pT = [carve(54272 + 1024 * i, 1024, BF16) for i in range(4)]
        osb = carve(58368, 2048)
        rden = carve(60416, 2048)
        ohp = carve(51200, 1536, F32, parts=32)
        Fs = carve(52736, 1536, F32, parts=8)
        hk = [carve(54272 + 512 * i, 512) for i in range(2)]
        rbt = carve(55296, 32, F32, parts=32)
        woA = carve(0, 16384, BF16, parts=64).rearrange("p (h n) -> p h n", h=8)
        woR = carve(16384, 8192, BF16).rearrange("p (h n) -> p h n", h=4)

        ident = sb("identf", [128, 128]); identb = sb("identb", [128, 128], BF16)
        tri = sb("tri", [128, 128]); jmat = sb("jmat", [128, 128]); onesf = sb("onesf", [128, 128])
        GB = sb("GB", [128, 2, D])
        gng = sb("gng", [128, 512])
        xtm = sb("xtm", [128, NB, D])
        kT = sb("kT", [128, 4, SEQ], BF16)
        kiT2 = sb("kiT2", [128, SEQ], BF16)
        vflat = sb("vaug", [128, (SEQ // 128) * 8 * 65 + 64], BF16)
        vaug = vflat[:, 0:(SEQ // 128) * 8 * 65].rearrange("p (j h c) -> p j h c", j=SEQ // 128, h=8)
        qT = sb("qT", [128, 4, T], BF16); qiT = sb("qiT", [128, 4, T], BF16)
        wi = sb("wi", [128, NB, 8])
        mixA = sb("mixA", [64, 8, T], BF16); mixR = sb("mixR", [128, 4, T], BF16)
        BT = sb("BT", [128, 16, 128], BF16)
        cfar = sb("cfar", [128, 8])
        intra = sb("intra", [128, 512]); cdt = sb("cdt", [128, 512])
        kvd = sb("kvd", [128, 512]); kvdp = sb("kvdp", [128, 512]); pbias = sb("pbias", [128, 1])
        S = sb("S", [128, 512]); Sb = sb("Sb", [128, 512], BF16)
        stat = sb("stat", [128, NB, 2, 6]); mv = sb("mv", [128, NB, 2]); rstd = sb("rstd", [128, NB, 2])
        gst = sb("gst", [128, 4, 6]); gmv = sb("gmv", [128, 4, 2]); grs = sb("grs", [128, 4, 2])
        bs = sb("bs", [128, 8])

        for dst, src in ((ident, ident_d), (tri, tri_d), (jmat, jmat_d), (intra, intra_d), (cdt, cdt_d),
                         (kvd, kvd_d), (kvdp, kvdp_d), (pbias, pbias_d), (rbt, relb), (ohp, ohp_d)):
            P.dma("sp", dst[:], src)
        P.dma("sp", gng[:], gng_d.partition_broadcast(128))
        P.dma("sp", cfar[:], relb[31].partition_broadcast(128))
        P.copy("dve", identb[:], ident[:])
        P.memset("pool", onesf[:], 1.0)
        P.memset("pool", S[:], 0.0)
        P.memset("pool", Sb[:], 0.0)
        P.memset("pool", vflat[:, :], 0.0)
        P.memset("pool", vaug[:, :, :, 64:65], 1.0)

        P.mm(ps[0][0:8, 0:384], rbt[:, :], ohp[:, :])
        P.copy("dve", Fs[:], ps[0][0:8, 0:384])
        P.dma("sp", fscr, Fs[:])
        for hd in range(8):
            for v in range(2):
                n = hd * 2 + v
                P.add("sp", (lambda e, n=n, hd=hd, v=v: e.dma_start(
                    hk[n % 2][:], bass.AP(fscr.tensor, hd * 384 + v * 128, [[1, 128], [1, 128]]))),
                    rkeys=[(fscr.tensor.name, 0)], writes=[hk[n % 2][:]], dma=True)
                P.mm(ps[1 + n % 2][:, 0:128], jmat[:], hk[n % 2][:])
                P.copy("dve", BT[:, n, :], ps[1 + n % 2][:, 0:128])

        cast_rr = [0]
        jobs = []

        def prep_gu(w, sdst):
            for kc in range(8):
                jobs.append((w[kc * 128:(kc + 1) * 128, :], 128, DFF,
                             [(lambda sbv, kc=kc, sdst=sdst: (sdst[:, :, kc, :].rearrange("c p f -> p c f"),
                                                              sbv[:, 0:DFF].rearrange("p (c f) -> p c f", f=128)))]))

        def prep_d(w, sdst):
            for c in range(NFC):
                jobs.append((w[c * 128:(c + 1) * 128, :], 128, D,
                             [(lambda sbv, c=c, sdst=sdst: (sdst[c], sbv[:, 0:D]))]))

        prep_gu(w1g, s1g)
        prep_gu(w1u, s1u)
        prep_d(w1d, s1d)
        for kc in range(8):
            outs = []
            for ci, (c0, w) in enumerate(zip(FM_CHUNKS, FM_W)):
                outs.append(lambda sbv, ci=ci, c0=c0, w=w, kc=kc: (sfm[ci, :, kc, 0:w], sbv[:, c0:c0 + w]))
            outs.append(lambda sbv, kc=kc: (sfm[12, :, kc, 64:128], sbv[:, C_KI:C_KI + 64]))
            for gi, (c0, w) in enumerate(TM_GROUPS):
                outs.append(lambda sbv, gi=gi, c0=c0, w=w, kc=kc: (stm[gi, :, kc, 0:w], sbv[:, c0:c0 + w]))
            jobs.append((win[kc * 128:(kc + 1) * 128, :], 128, NIN, outs))
        for hd in range(8):
            jobs.append((wout[hd * 64:(hd + 1) * 64, :], 64, D, [(lambda sbv, hd=hd: (swoA[:, hd, :], sbv[0:64, 0:D]))]))
        for h in range(4):
            jobs.append((wout[512 + h * 128:512 + (h + 1) * 128, :], 128, D,
                         [(lambda sbv, h=h: (swoR[:, h, :], sbv[:, 0:D]))]))
        prep_gu(w2g, s2g)
        prep_gu(w2u, s2u)
        prep_d(w2d, s2d)

        def job_in(k):
            src, parts, n, _ = jobs[k]
            P.dma("sp", stages[k % 2][0:parts, 0:n], src)

        job_in(0)
        for k, (src, parts, n, outs) in enumerate(jobs):
            if k + 1 < len(jobs):
                job_in(k + 1)
            e = ("dve", "act", "pool")[k % 3]
            P.copy(e, stagebs[k % 2][0:parts, 0:n], stages[k % 2][0:parts, 0:n])
            for fn in outs:
                dst, srcv = fn(stagebs[k % 2])
                P.dma("sp", dst, srcv, waw=False)

        cur = {"nb": NB}

        def load_x(xsrc):
            nb = cur["nb"]
            P.dma("sp", xtm[:, 0:nb, :], xsrc.rearrange("(b p) d -> p b d", p=128))

        def transpose_x():
            nb = cur["nb"]; nt = nb * 128
            for kc in range(8):
                bank = ps[kc % 4]
                for b in range(nb):
                    P.tr(bank[:, b * 128:(b + 1) * 128], xtm[:, b, kc * 128:(kc + 1) * 128], ident[:])
                if kc % 2 == 0:
                    P.copy("dve", xT[:, kc, 0:nt], bank[:, 0:nt])
                else:
                    P.act(xT[:, kc, 0:nt], bank[:, 0:nt], AF.Copy)

        def ln_blocks(lng, lnb):
            P.dma("sp", GB[:, 0, :], lng.partition_broadcast(128))
            P.dma("sp", GB[:, 1, :], lnb.partition_broadcast(128))
            for b in range(cur["nb"]):
                for h in range(2):
                    P.add("dve", (lambda e, b=b, h=h: e.bn_stats(stat[:, b, h, :], xtm[:, b, h * 512:(h + 1) * 512])),
                          reads=[xtm[:, b, h * 512:(h + 1) * 512]], writes=[stat[:, b, h, :]])
                P.add("dve", (lambda e, b=b: e.bn_aggr(mv[:, b, :], stat[:, b, :, :])),
                      reads=[stat[:, b, :, :]], writes=[mv[:, b, :]])
                P.ts("dve", rstd[:, b, 0:1], mv[:, b, 1:2], EPS, None, ALU.add)
                P.act(rstd[:, b, 0:1], rstd[:, b, 0:1], AF.Sqrt)
                P.add("dve", (lambda e, b=b: e.reciprocal(rstd[:, b, 0:1], rstd[:, b, 0:1])),
                      reads=[rstd[:, b, 0:1]], writes=[rstd[:, b, 0:1]])
                P.stt("dve", rstd[:, b, 1:2], mv[:, b, 0:1], -1.0, rstd[:, b, 0:1], ALU.mult, ALU.mult)
                P.act(xtm[:, b, :], xtm[:, b, :], AF.Identity, bias=rstd[:, b, 1:2], scale=rstd[:, b, 0:1])
                P.tt("pool", xtm[:, b, :], xtm[:, b, :], GB[:, 0, :], ALU.mult)
                P.tt("pool", xtm[:, b, :], xtm[:, b, :], GB[:, 1, :], ALU.add)

        def ffn_core(sg_, su_, sd_, lng, lnb):
            nb = cur["nb"]; nt = nb * 128
            for c in range(NFC):
                s = c % 3
                P.dma("sp", wgb[s][:], sg_[c])
                P.dma("sp", wub[s][:], su_[c])
                pg = ps[(c % 2) * 2]
                pu = ps[(c % 2) * 2 + 1]
                for kc in range(8):
                    P.mm(pg[:, 0:nt], wgb[s][:, kc, :], xT[:, kc, 0:nt], start=(kc == 0), stop=(kc == 7))
                for kc in range(8):
                    P.mm(pu[:, 0:nt], wub[s][:, kc, :], xT[:, kc, 0:nt], start=(kc == 0), stop=(kc == 7))
                P.act(sg[c % 2][:, 0:nt], pg[:, 0:nt], AF.Silu)
                P.stt("dve", actb[:, c, 0:nt], pu[:, 0:nt], 0.5, sg[c % 2][:, 0:nt], ALU.mult, ALU.mult)
            for c in range(NFC):
                s = c % 3
                P.dma("sp", wdr[s][:], sd_[c])
                for b in range(nb):
                    for h in range(2):
                        P.mm(ps[b * 2 + h][:, :], actb[:, c, b * 128:(b + 1) * 128],
                             wdr[s][:, h * 512:(h + 1) * 512], start=(c == 0), stop=(c == NFC - 1))
            for b in range(nb):
                for h in range(2):
                    P.stt("dve", xtm[:, b, h * 512:(h + 1) * 512], xtm[:, b, h * 512:(h + 1) * 512], ALPHA,
                          ps[b * 2 + h][:, :], ALU.mult, ALU.add)
            ln_blocks(lng, lnb)

        def win_fm(tok0, own):
            chunks = range(13) if own else [4, 5, 6, 7, 12]
            for n, ci in enumerate(chunks):
                s = n % 3
                P.dma("sp", fmw[s][:], sfm[ci])
                bank = ps[n % 4]
                for kc in range(8):
                    P.mm(bank[:, :], fmw[s][:, kc, :], hT[:, kc, :], start=(kc == 0), stop=(kc == 7))
                if ci < 4:
                    P.act(qT[:, ci, :], bank[:, :], AF.Copy, scale=0.125)
                elif ci < 8:
                    P.copy("dve", kT[:, ci - 4, tok0:tok0 + T], bank[:, :])
                elif ci < 12:
                    P.act(qiT[:, ci - 8, :], bank[:, :], AF.Copy)
                else:
                    P.copy("dve", kiT2[:, tok0:tok0 + T], bank[:, :])

        def rotary(out, use_k, rp):
            x3 = rx[:, :].rearrange("p (h f) -> p h f", h=4)
            o3 = out.rearrange("p (h f) -> p h f", h=4)
            co = rp[:, 2 if use_k else 0, :].unsqueeze(1).broadcast_to([128, 4, 64])
            si = rp[:, 3 if use_k else 1, :].unsqueeze(1).broadcast_to([128, 4, 64])
            t3 = [tt_[:, :].rearrange("p (h f) -> p h f", h=4) for tt_ in tmp]
            x1 = x3[:, :, 0:64]
            x2 = x3[:, :, 64:128]
            P.tt("dve", t3[0], x1, co, ALU.mult)
            P.tt("pool", t3[1], x2, si, ALU.mult)
            P.tt("pool", t3[2], x1, si, ALU.mult)
            P.tt("dve", t3[3], x2, co, ALU.mult)
            P.tt("dve", o3[:, :, 0:64], t3[0], t3[1], ALU.subtract)
            P.tt("pool", o3[:, :, 64:128], t3[2], t3[3], ALU.add)

        def win_tm(row0, gblk0, own, rope_src):
            groups = list(globals().get("GROUPS", [0, 1, 2, 3, 4, 5, 6])) if own else [1, 4, 5]
            n = 0
            for gi in groups:
                c0, w = TM_GROUPS[gi]
                P.dma("sp", tmw[:, :, 0:w], stm[gi][:, :, 0:w])
                for b in range(NB):
                    bank = ps[4 + (n % 4)]
                    n += 1
                    for kc in range(8):
                        P.mm(bank[:, 0:w], hT[:, kc, b * 128:(b + 1) * 128], tmw[:, kc, 0:w],
                             start=(kc == 0), stop=(kc == 7))
                    st = ost[n % 3]
                    r0 = row0 + b * 128
                    if gi == 0:
                        P.act(st[:, :], bank[:, :], AF.Copy)
                        P.dma(OUTQ, o_k[r0:r0 + 128, :], st[:, :])
                    elif gi == 1:
                        if own:
                            P.copy("dve", st[:, :], bank[:, :])
                            P.dma(OUTQ, o_v[r0:r0 + 128, :], st[:, :])
                        P.act(vaug[:, gblk0 + b, :, 0:64], bank[:, :].rearrange("p (h d) -> p h d", h=8), AF.Copy)
                    elif gi == 2:
                        P.copy("dve", st[:, 0:64], bank[:, 0:64])
                        P.dma(OUTQ, o_ki[r0:r0 + 128, :], st[:, 0:64])
                        P.ts("dve", wi[:, b, :], bank[:, 64:72], WSCALE, None, ALU.mult)
                    elif gi in (3, 4):
                        rp = ropes[(n) % 2]
                        P.dma("sp", rp[:], rope_src[r0:r0 + 128])
                        P.act(rx[:, :], bank[:, :], AF.Copy)
                        rotary((qrot if gi == 3 else krot)[:, b, :], gi == 4, rp)
                    elif gi == 5:
                        P.act(vrb[:, b, :], bank[:, :], AF.Copy)
                    else:
                        P.act(sgr[:, b, :], bank[:, :], AF.Silu)

        tb = ps[3][:, :].bitcast(BF16)

        SUB = int(globals().get("SUB", 9))

        def ret_block(b, tcol0, own):
            hc = lambda h: slice(h * 128, (h + 1) * 128)
            if own and SUB < 1:
                return
            if own:
                for h in range(4):
                    P.tr(tb[:, hc(h)], qrot[:, b, hc(h)], identb[:])
                for h in range(4):
                    P.tr(tb[:, 512 + h * 128:512 + (h + 1) * 128], krot[:, b, hc(h)], identb[:])
                P.act(qkT[:, :], tb[:, :], AF.Copy)
                P.tt("pool", qpT[:, :], qkT[:, 0:512], cdt[:, :], ALU.mult)
                for h in range(4):
                    P.mm(ps[0][:, hc(h)], qkT[:, 512 + h * 128:512 + (h + 1) * 128], qkT[:, hc(h)])
                P.tt("dve", attm[:, :], ps[0][:, :], intra[:, :], ALU.mult)
                for h in range(4):
                    P.mm(ps[1][:, hc(h)], attm[:, hc(h)], vrb[:, b, hc(h)], start=True, stop=False)
                    P.mm(ps[1][:, hc(h)], qpT[:, hc(h)], Sb[:, hc(h)], start=False, stop=True)
            P.tt("pool", kd[:, :], krot[:, b, :], (kvd if own else kvdp)[:, :], ALU.mult)
            for h in range(4):
                P.mm(ps[2][:, hc(h)], kd[:, hc(h)], vrb[:, b, hc(h)])
            for h in range(4):
                P.stt("dve", S[:, hc(h)], S[:, hc(h)], CDK[h], ps[2][:, hc(h)], ALU.mult, ALU.add)
            P.act(Sb[:, :], S[:, :], AF.Copy)
            if not own or SUB < 2:
                return
            ret_tail(b, tcol0)

        def ret_tail(b, tcol0):
            hc = lambda h: slice(h * 128, (h + 1) * 128)
            for h in range(4):
                P.add("dve", (lambda e, h=h: e.bn_stats(gst[:, h, :], ps[1][:, hc(h)])),
                      reads=[ps[1][:, hc(h)]], writes=[gst[:, h, :]])
                P.add("dve", (lambda e, h=h: e.bn_aggr(gmv[:, h, :], gst[:, h, :])),
                      reads=[gst[:, h, :]], writes=[gmv[:, h, :]])
            P.ts("dve", grs[:, :, 0], gmv[:, :, 1], EPS, None, ALU.add)
            P.act(grs[:, :, 0], grs[:, :, 0], AF.Sqrt)
            P.add("dve", (lambda e: e.reciprocal(grs[:, :, 0], grs[:, :, 0])),
                  reads=[grs[:, :, 0]], writes=[grs[:, :, 0]])
            P.stt("dve", grs[:, :, 1], gmv[:, :, 0], -1.0, grs[:, :, 0], ALU.mult, ALU.mult)
            for h in range(4):
                P.act(yn[:, hc(h)], ps[1][:, hc(h)], AF.Identity, bias=grs[:, h, 1:2], scale=grs[:, h, 0:1])
            P.tt("pool", yn[:, :], yn[:, :], gng[:, :], ALU.mult)
            P.tt("pool", ytm[:, :], yn[:, :], sgr[:, b, :], ALU.mult)
            if SUB < 3:
                return
            for h in range(4):
                P.tr(tb[:, hc(h)], ytm[:, hc(h)], identb[:])
            P.copy("dve", mixR[:, :, tcol0:tcol0 + 128], tb[:, 0:512].rearrange("p (h t) -> p h t", h=4))

        def indexer(t, i):
            qb = 16 + t * NB + i
            W = (qb + 1) * 128
            for h in range(8):
                P.ts("dve", Dg[:, h, :], identb[:], wi[:, i, h:h + 1], None, ALU.mult)
            nch = (W + 511) // 512
            for c in range(nch):
                k0 = c * 512
                w = min(512, W - k0)
                accb = ps[3 + (c % 2)]
                def dots(h):
                    par, ch = h % 2, h // 2
                    P.mm(ps[h % 3][:, 0:w], qiT[par * 64:(par + 1) * 64, ch, i * 128:(i + 1) * 128],
                         kiT2[par * 64:(par + 1) * 64, k0:k0 + w])
                dots(0)
                dots(1)
                for h in range(8):
                    if h + 2 < 8:
                        dots(h + 2)
                    P.act(Rr[h % 3][:, 0:w], ps[h % 3][:, 0:w], AF.Relu)
                    P.mm(accb[:, 0:w], Dg[:, h, :], Rr[h % 3][:, 0:w], start=(h == 0), stop=(h == 7))
                last = (c == nch - 1)
                wf = w - 128 if last else w
                if wf > 0:
                    if k0 < HALF:
                        P.ts("dve", sc[:, k0:k0 + wf], accb[:, 0:wf], pbias[:, 0:1], None, ALU.add)
                    else:
                        P.act(sc[:, k0:k0 + wf], accb[:, 0:wf], AF.Copy)
                if last:
                    P.tt("dve", sc[:, W - 128:W], accb[:, w - 128:w], tri[:, :], ALU.add)
            mx, lo, mid, cnt, geh, flag, thr = [bs[:, k:k + 1] for k in range(7)]
            P.add("dve", (lambda e: e.reduce_max(mx, sc[:, 0:W], AX.X)), reads=[sc[:, 0:W]], writes=[mx])
            P.ts("dve", mid, mx, -R0 + R0 / 2.0, None, ALU.add)
            for k in range(NIT):
                half = R0 / (2.0 ** (k + 1))
                P.ts("dve", MB[i][:, 0:W], sc[:, 0:W], mid, None, ALU.is_ge, op1=ALU.add, accum_out=cnt)
                P.ts("dve", geh, cnt, 255.5, half, ALU.is_ge, ALU.mult)
                P.stt("dve", mid, geh, -half / 2.0, mid, ALU.add, ALU.add)
            hK = R0 / (2.0 ** (NIT + 1))
            P.ts("dve", MB[i][:, 0:W], sc[:, 0:W], -1.0e29, None, ALU.is_ge, op1=ALU.add, accum_out=cnt)
            P.ts("dve", flag, cnt, 255.5, 1.0e29, ALU.is_lt, ALU.mult)
            P.stt("dve", thr, mid, -hK, flag, ALU.add, ALU.subtract)
            P.ts("dve", MB[i][:, 0:W], sc[:, 0:W], thr, None, ALU.is_ge)
            if debug:
                P.dma(OUTQ, d_thr[t * NB + i], bs[:, 0:7])
                if t == NT - 1 and i == NB - 1:
                    P.dma(OUTQ, d_sc, sc[:, 0:SEQ])

        def attention(t):
            qb0 = 16 + t * NB
            jmax = qb0 + NB - 1
            MT = [Rr[0], Rr[1]]
            cnt = [0]
            for hh in range(2):
                its = []
                for j in range(jmax + 1):
                    for h4 in range(4):
                        its.append((j, h4, cnt[0]))
                        cnt[0] += 1

                def emit_mt(j):
                    imin = max(0, j - qb0)
                    c0 = imin * 128
                    half = (j % 2) * 512
                    for i in range(imin, NB):
                        P.tr(tb[:, half + i * 128:half + (i + 1) * 128], MB[i][:, j * 128:(j + 1) * 128], identb[:])
                    P.copy("dve", MT[j % 2][:, c0:T], tb[:, half + c0:half + T])

                def emit_logits(it):
                    j, h4, n = it
                    hd = hh * 4 + h4
                    par, ch = hd % 2, hd // 2
                    imin = max(0, j - qb0)
                    c0 = imin * 128
                    L = ps[n % 3]
                    mms = [(L[:, c0:T], kT[par * 64:(par + 1) * 64, ch, j * 128:(j + 1) * 128],
                            qT[par * 64:(par + 1) * 64, ch, c0:T])]
                    for i in range(imin, NB):
                        dd = qb0 + i - j
                        if dd in (0, 1):
                            mms.append((L[:, i * 128:(i + 1) * 128], identb[:, :], BT[:, hd * 2 + dd, :]))
                    for m, (o_, l_, r_) in enumerate(mms):
                        P.mm(o_, l_, r_, start=(m == 0), stop=(m == len(mms) - 1))

                def emit_pv(it):
                    j, h4, n = it
                    hd = hh * 4 + h4
                    imin = max(0, j - qb0)
                    c0 = imin * 128
                    ncol = T - c0
                    L = ps[n % 3]
                    pt = pT[n % 4]
                    P.act(pt[:, 0:ncol], L[:, c0:T], AF.Exp, bias=cfar[:, hd:hd + 1])
                    P.tt("dve", pt[:, 0:ncol], pt[:, 0:ncol], MT[j % 2][:, c0:T], ALU.mult)
                    v0 = (j * 8 + hd) * 65
                    P.mm(ps[4 + h4][:, c0:T], vflat[:, v0:v0 + 128], pt[:, 0:ncol],
                         start=(j == 0), stop=(j == jmax))

                emit_mt(0)
                emit_logits(its[0])
                emit_logits(its[1])
                for k, it in enumerate(its):
                    j, h4, n = it
                    if h4 == 0 and j + 1 <= jmax:
                        emit_mt(j + 1)
                    if k + 2 < len(its):
                        emit_logits(its[k + 2])
                    emit_pv(it)
                for h4 in range(4):
                    hd = hh * 4 + h4
                    P.act(osb[0:65, :], ps[4 + h4][0:65, :], AF.Copy)
                    P.add("dve", (lambda e: e.reciprocal(rden[64:65, :], osb[64:65, :])),
                          reads=[osb[64:65, :]], writes=[rden[64:65, :]])
                    P.mm(ps[3][0:64, :], onesf[64:65, 0:64], rden[64:65, :])
                    P.tt("dve", mixA[:, hd, :], osb[0:64, :], ps[3][0:64, :], ALU.mult)

        def wout_ln2():
            P.dma("sp", woA[:, :, :], swoA)
            P.dma("sp", woR[:, :, :], swoR)
            for b in range(cur["nb"]):
                bc = slice(b * 128, (b + 1) * 128)
                for h2 in range(2):
                    acc = ps[b * 2 + h2]
                    hcs = slice(h2 * 512, (h2 + 1) * 512)
                    for hd in range(8):
                        P.mm(acc[:, :], mixA[:, hd, bc], woA[:, hd, hcs], start=(hd == 0), stop=False)
                    for h in range(4):
                        P.mm(acc[:, :], mixR[:, h, bc], woR[:, h, hcs], start=False, stop=(h == 3))
                    P.stt("dve", xtm[:, b, hcs], xtm[:, b, hcs], ALPHA, acc[:, :], ALU.mult, ALU.add)
            ln_blocks(ln2g, ln2b)


        GAM = [math.exp(LG[h]) for h in range(4)]

        def sample_path():
            hc = lambda h: slice(h * 128, (h + 1) * 128)
            cur["nb"] = 1
            qm = carve(0, 4096, BF16).rearrange("p (h r t) -> p h r t", h=4, r=4)
            S0 = carve(4096, 2048); Sn = carve(6144, 2048)
            Snb = [carve(8192 + 1024 * r, 1024, BF16) for r in range(4)]
            kdr = carve(12288, 1024, BF16)
            kin = carve(13312, 256, BF16, parts=64)
            kas = carve(22528, 2048); vas = carve(24576, 2048)
            qis = carve(26624, 2048, BF16, parts=64).rearrange("p (h t) -> p h t", h=8)
            KPqs = [carve(0, 8192, F32, parts=64), carve(8192, 8192, F32, parts=64)]
            Ksel = carve(8192, 4096).rearrange("p (a f) -> p a f", a=2)
            Vsel = carve(12288, 4096).rearrange("p (a f) -> p a f", a=2)
            ltri_t = carve(16640, 512); iq = carve(17152, 1024); rowtab = carve(18176, 260); lstab = carve(18688, 260)
            tbt = carve(19200, 128)
            kiTs = carve(28672, 16640, BF16, parts=64)
            selrow = [carve(45312 + 512 * r, 512) for r in range(4)]
            sm = carve(47360, 6144)
            smi = sm.bitcast(I32)
            sc_all = carve(53504, 1040).rearrange("p (r c) -> p r c", r=4)
            m_all = carve(54592, 1040).rearrange("p (r c) -> p r c", r=4)
            OHs = [carve(55680, 1024), carve(60928, 1024)]
            OH = OHs[0]
            Rs = carve(56704, 2080); tmpS = carve(58816, 2080)
            tmpK = carve(56704, 2048)
            v = lambda a, n=1: sm[:, a:a + n]
            ptf, pt128, base, rowtot = v(0), v(1), v(2), v(3)
            lo4, mid4, cnt4, geh4, pm4 = v(8, 4), v(12, 4), v(16, 4), v(20, 4), v(24, 4)
            csA, csB, rank, phys = v(32, 65), v(97, 65), v(162, 65), v(227, 65)
            vals = v(292, 130).rearrange("p (c k) -> p c k", k=2)
            sel = v(422, 4).rearrange("p (a k) -> p a k", k=2)
            flag, nn = v(426, 2), v(428, 2)
            lg = v(430, 16).rearrange("p (a h) -> p a h", a=2)
            ee = v(446, 16).rearrange("p (a h) -> p a h", a=2)
            biasv, steps, wbs = v(462, 8), v(470, 32), v(502, 8)
            rdv = sm[0:64, 510:518]
            mcol = v(518)
            gm = sm[0:4, 521:522]; dgm = sm[0:4, 522:526]
            idx32 = smi[:, 526:528]; pt32 = smi[:, 528:529]
            tmpb = v(530, 256).rearrange("p (h b) -> p h b", h=8)
            drb = v(786, 256).rearrange("p (b h) -> p b h", h=8)
            RBb = v(1042, 256).rearrange("p (b h) -> p b h", h=8)

            load_x(xs_d)
            transpose_x()
            ffn_core(s1g, s1u, s1d, ln1g, ln1b)
            transpose_x()

            def tm_group(gi, bank):
                c0, w = TM_GROUPS[gi]
                P.dma("sp", tmw[:, :, 0:w], stm[gi][:, :, 0:w])
                for kc in range(8):
                    P.mm(bank[:, 0:w], hT[:, kc, 0:128], tmw[:, kc, 0:w], start=(kc == 0), stop=(kc == 7))

            bank = ps[4]
            tm_group(0, bank)
            P.act(kas[:, :], bank[:, :], AF.Copy)
            P.dma(OUTQ, o_ks, kas[0:4, :])
            bank = ps[5]
            tm_group(1, bank)
            P.act(vas[:, :], bank[:, :], AF.Copy)
            P.dma(OUTQ, o_vs, vas[0:4, :])
            bank = ps[6]
            tm_group(2, bank)
            P.copy("dve", rx[:, 0:64], bank[:, 0:64])
            P.dma(OUTQ, o_kis, rx[0:4, 0:64])
            P.ts("dve", wi[:, 0, :], bank[:, 64:72], WSCALE, None, ALU.mult)
            for n, ci in enumerate((0, 1, 2, 3)):
                sl = n % 3
                P.dma("sp", fmw[sl][:], sfm[ci])
                bank = ps[n % 4]
                for kc in range(8):
                    P.mm(bank[:, 0:128], fmw[sl][:, kc, :], hT[:, kc, 0:128], start=(kc == 0), stop=(kc == 7))
                P.act(qT[:, ci, 0:128], bank[:, 0:128], AF.Copy, scale=0.125)
            for n, ci in enumerate((8, 9, 10, 11, 12)):
                sl = (n + 1) % 3
                P.dma("sp", fmw[sl][:], sfm[ci])
                bank = ps[n % 4]
                for hf in range(2):
                    for kc in range(8):
                        P.mm(bank[0:64, hf * 128:(hf + 1) * 128], fmw[sl][:, kc, hf * 64:(hf + 1) * 64], hT[:, kc, 0:128],
                             start=(kc == 0), stop=(kc == 7))
                if ci < 12:
                    P.act(qis[:, 2 * (ci - 8):2 * (ci - 8) + 2, :], bank[0:64, 0:256].rearrange("p (a t) -> p a t", a=2), AF.Copy)
                else:
                    P.act(kin[:, :], bank[0:64, 0:128], AF.Copy)

            P.dma("sp", ropes[0][:], rope_s)
            for n, gi in enumerate((3, 4, 5, 6)):
                bank = ps[4 + n]
                tm_group(gi, bank)
                if gi in (3, 4):
                    P.act(rx[:, :], bank[:, :], AF.Copy)
                    rotary((qrot if gi == 3 else krot)[:, 0, :], gi == 4, ropes[0])
                elif gi == 5:
                    P.act(vrb[:, 0, :], bank[:, :], AF.Copy)
                else:
                    P.act(sgr[:, 0, :], bank[:, :], AF.Silu)
            for h in range(4):
                P.tr(tb[:, hc(h)], qrot[:, 0, hc(h)], identb[:])
            P.act(qkT[:, 0:512], tb[:, 0:512], AF.Copy)
            P.memset("pool", qm[:, :, :, :], 0.0)
            q3 = qkT[:, 0:512].rearrange("p (h t) -> p h t", h=4)
            for r in range(4):
                P.copy("dve", qm[:, :, r, r:r + 1], q3[:, :, r:r + 1])
            for r in range(4):
                P.dma("sp", S0[:, :].rearrange("p (h v) -> p h v", h=4), state_d[r].rearrange("h k v -> k h v"))
                P.ts("pool", kdr[:, :], krot[:, 0, :], ident[:, r:r + 1], None, ALU.mult)
                for h in range(4):
                    P.mm(ps[2][:, hc(h)], kdr[:, hc(h)], vrb[:, 0, hc(h)])
                for h in range(4):
                    P.stt("dve", Sn[:, hc(h)], S0[:, hc(h)], GAM[h], ps[2][:, hc(h)], ALU.mult, ALU.add)
                P.dma(OUTQ, o_Ss[r], Sn[:, :])
                P.act(Snb[r][:, :], Sn[:, :], AF.Copy)
            for h in range(4):
                for r in range(4):
                    P.mm(ps[1][:, hc(h)], qm[:, h, r, :], Snb[r][:, hc(h)], start=(r == 0), stop=(r == 3))
            ret_tail(0, 0)

            for dst, src in ((ltri_t, ltri_d), (iq, iotaq_d), (rowtab, rowtab_d), (lstab, lstab_d), (tbt, tbt_d)):
                P.dma("sp", dst[:, :], src)
            P.dma("sp", RBb[:, :, :].rearrange("p b h -> p (b h)"), relb.rearrange("b h -> (b h)").partition_broadcast(128))
            P.copy("dve", drb[:, 0:1, :], RBb[:, 0:1, :])
            P.tt("dve", drb[:, 1:32, :], RBb[:, 1:32, :], RBb[:, 0:31, :], ALU.subtract)
            for r in range(4):
                P.copy("dve", selrow[r][:, :], ident[:, r:r + 1].broadcast_to([128, 128]))
            P.copy("dve", vals[:, :, 1], lstab[:, :])

            P.copy("dve", kiTs[:, 8192:8320], kin[:, :])
            for r in range(4):
                P.dma("sp", smi[0:64, 528:529], pt_d[r].rearrange("(p o) -> p o", o=1))
                ev = 0
                for qq in range(4):
                    KPq = KPqs[qq % 2]
                    qix = smi[0:64, 529 + (qq % 2):530 + (qq % 2)]
                    P.ts("dve", qix, smi[0:64, 528:529], 4, qq, ALU.mult, ALU.add)
                    P.add("pool", (lambda e, KPq=KPq, qix=qix: e.indirect_dma_start(
                        KPq[:, :], None, ckidx_d, bass.IndirectOffsetOnAxis(ap=qix, axis=0))),
                        reads=[qix], writes=[KPq[:, :]], dma=True)
                    for r8 in range(4):
                        bank = ps[ev % 4]
                        for k in range(8):
                            row = r8 * 8 + k
                            P.tr(bank[0:64, k * 64:(k + 1) * 64], KPq[:, row * 64:(row + 1) * 64], ident[0:64, 0:64])
                        col0 = (qq * 32 + r8 * 8) * 64
                        if ev % 2 == 0:
                            P.act(kiTs[:, col0:col0 + 512], bank[0:64, :], AF.Copy)
                        else:
                            P.copy("dve", kiTs[:, col0:col0 + 512], bank[0:64, :])
                        ev += 1
                for c in range(65):
                    if c < 64:
                        o_ = ps[4][:, c * 8:(c + 1) * 8]
                    else:
                        o_ = ps[5][:, 0:8]
                    P.mm(o_, kiTs[:, c * 128:(c + 1) * 128], qis[:, :, r])
                P.act(Rs[:, 0:512], ps[4][:, :], AF.Relu)
                P.act(Rs[:, 512:520], ps[5][:, 0:8], AF.Relu)
                P.mm(ps[6][:, 0:8], selrow[r][:, :], wi[:, 0, :])
                P.copy("dve", wbs, ps[6][:, 0:8])
                R3 = Rs[:, 0:520].rearrange("p (c h) -> p c h", h=8)
                T3 = tmpS[:, 0:520].rearrange("p (c h) -> p c h", h=8)
                P.tt("dve", T3, R3, wbs.unsqueeze(1).broadcast_to([128, 65, 8]), ALU.mult)
                P.add("dve", (lambda e, r=r, T3=T3: e.tensor_reduce(sc_all[:, r, :], T3, AX.X, ALU.add)),
                      reads=[T3], writes=[sc_all[:, r, :]])
                P.ts("dve", mcol, ident[:, r:r + 1], -1.0, 1.0e30, ALU.add, ALU.mult)
                P.tt("dve", sc_all[:, r, 64:65], sc_all[:, r, 64:65], mcol, ALU.add)

            for r in range(4):
                P.add("dve", (lambda e, r=r: e.reduce_max(pm4[:, r:r + 1], sc_all[:, r, :], AX.X)),
                      reads=[sc_all[:, r, :]], writes=[pm4[:, r:r + 1]])
            P.tr(ps[7][0:4, 0:128], pm4, ident[:, :])
            P.add("dve", (lambda e: e.reduce_max(gm, ps[7][0:4, 0:128], AX.X)), reads=[ps[7][0:4, 0:128]], writes=[gm])
            P.ts("dve", dgm, ident[0:4, 0:4], gm, None, ALU.mult)
            P.mm(ps[7][:, 128:132], onesf[0:4, :], dgm)
            P.ts("dve", mid4, ps[7][:, 128:132], -R0 + R0 / 2.0, None, ALU.add)
            for k in range(NIT):
                half = R0 / (2.0 ** (k + 1))
                for r in range(4):
                    P.ts("dve", tmpS[:, 0:65], sc_all[:, r, :], mid4[:, r:r + 1], None, ALU.is_ge, op1=ALU.add,
                         accum_out=cnt4[:, r:r + 1])
                P.mm(ps[7][:, 136:140], onesf[:, :], cnt4)
                P.ts("dve", geh4, ps[7][:, 136:140], 255.5, half, ALU.is_ge, ALU.mult)
                P.stt("dve", mid4, geh4, -half / 2.0, mid4, ALU.add, ALU.add)
            P.ts("dve", lo4, mid4, -R0 / (2.0 ** (NIT + 1)), None, ALU.add)
            for r in range(4):
                P.ts("dve", m_all[:, r, :], sc_all[:, r, :], lo4[:, r:r + 1], None, ALU.is_ge)

            P.memset("pool", mixA[:, :, 0:128], 0.0)
            for r in range(4):
                m = m_all[:, r, :]
                P.add("dve", (lambda e, m=m: e.reduce_sum(rowtot, m, AX.X)), reads=[m], writes=[rowtot])
                src = m
                bufs = [csA, csB]
                sh = 1
                nb_ = 0
                while sh < 65:
                    dst = bufs[nb_ % 2]
                    P.copy("dve", dst[:, 0:sh], src[:, 0:sh])
                    P.tt("dve", dst[:, sh:65], src[:, sh:65], src[:, 0:65 - sh], ALU.add)
                    src = dst
                    sh *= 2
                    nb_ += 1
                P.mm(ps[7][:, 144:145], ltri_t[:, :], rowtot)
                P.copy("dve", base, ps[7][:, 144:145])
                P.ts("dve", rank, src, base, None, ALU.add)
                P.dma("sp", smi[0:64, 528:529], pt_d[r].rearrange("(p o) -> p o", o=1))
                P.dma("sp", smi[64:128, 528:529], pt_d[r].rearrange("(p o) -> p o", o=1))
                P.copy("dve", ptf, pt32)
                P.ts("dve", pt128, ptf, 128.0, None, ALU.mult)
                P.ts("dve", phys, rowtab[:, :], pt128, None, ALU.add)
                P.memset("pool", phys[:, 64:65], 0.0)
                P.copy("dve", vals[:, :, 0], phys)
                for c in range(65):
                    OHc = OHs[c % 2]
                    P.ts("dve", OHc[:, :], iq[:, :], rank[:, c:c + 1], m[:, c:c + 1], ALU.is_equal, ALU.mult)
                    for a in range(2):
                        P.mm(ps[6 + a][:, 16:18], OHc[:, a * 128:(a + 1) * 128], vals[:, c, :],
                             start=(c == 0), stop=(c == 64))
                for a in range(2):
                    P.copy("dve", sel[:, a, :], ps[6 + a][:, 16:18])
                P.copy("dve", idx32, sel[:, :, 0])
                for a in range(2):
                    P.add("pool", (lambda e, a=a: e.indirect_dma_start(
                        Ksel[:, a, :], None, ck_d, bass.IndirectOffsetOnAxis(ap=idx32[:, a:a + 1], axis=0))),
                        reads=[idx32[:, a:a + 1]], writes=[Ksel[:, a, :]], dma=True)
                    P.add("pool", (lambda e, a=a: e.indirect_dma_start(
                        Vsel[:, a, :], None, cv_d, bass.IndirectOffsetOnAxis(ap=idx32[:, a:a + 1], axis=0))),
                        reads=[idx32[:, a:a + 1]], writes=[Vsel[:, a, :]], dma=True)
                P.ts("dve", flag, sel[:, :, 1], 8191.5, None, ALU.is_ge)
                P.ts("dve", nn, sel[:, :, 1], -1.0, 8192.0, ALU.mult, ALU.add)
                for (cache_sel, new_tm, pb) in ((Ksel, kas, ps[4]), (Vsel, vas, ps[5])):
                    P.mm(pb[:, :], selrow[r][:, :], new_tm[:, :])
                    for a in range(2):
                        P.tt("dve", tmpK[:, :], pb[:, :], cache_sel[:, a, :], ALU.subtract)
                        P.stt("dve", cache_sel[:, a, :], tmpK[:, :], flag[:, a:a + 1], cache_sel[:, a, :], ALU.mult, ALU.add)
                for ci in range(4):
                    P.copy("dve", OH[:, 0:64].bitcast(BF16), qT[:, ci, r:r + 1].broadcast_to([128, 128]))
                    P.mm(ps[3][:, ci * 128:(ci + 1) * 128], OH[:, 0:64].bitcast(BF16), identb[:, :])
                for a in range(2):
                    P.tt("dve", tmpK[:, :], Ksel[:, a, :], ps[3][:, :], ALU.mult)
                    P.add("dve", (lambda e, a=a: e.tensor_reduce(lg[:, a, :], tmpK[:, :].rearrange("p (h d) -> p h d", h=8),
                                                               AX.X, ALU.add)),
                          reads=[tmpK[:, :]], writes=[lg[:, a, :]])
                    P.ts("dve", steps, tbt[:, :], nn[:, a:a + 1], None, ALU.is_le)
                    P.tt("dve", tmpb, drb.rearrange("p b h -> p h b"), steps.unsqueeze(1).broadcast_to([128, 8, 32]), ALU.mult)
                    P.add("dve", (lambda e: e.tensor_reduce(biasv, tmpb, AX.X, ALU.add)), reads=[tmpb], writes=[biasv])
                    P.tt("dve", lg[:, a, :], lg[:, a, :], biasv, ALU.add)
                P.act(ee, lg, AF.Exp)
                for hd in range(8):
                    for a in range(2):
                        P.mm(ps[5][0:64, hd:hd + 1], Vsel[:, a, hd * 64:(hd + 1) * 64], ee[:, a, hd:hd + 1],
                             start=(a == 0), stop=(a == 1))
                for a in range(2):
                    P.mm(ps[6][0:64, 32:40], onesf[:, 0:64], ee[:, a, :], start=(a == 0), stop=(a == 1))
                P.add("dve", (lambda e: e.reciprocal(rdv, ps[6][0:64, 32:40])), reads=[ps[6][0:64, 32:40]], writes=[rdv])
                P.tt("dve", mixA[:, :, r], ps[5][0:64, 0:8], rdv, ALU.mult)

            wout_ln2()
            transpose_x()
            ffn_core(s2g, s2u, s2d, ln3g, ln3b)
            P.dma(OUTQ, o_ys, xtm[0:4, 0, :])
            cur["nb"] = NB

        STAGE = int(globals().get("STAGE", 6))
        TILES = int(globals().get("TILES", NT))
        for t in range(TILES if STAGE >= 1 else 0):
            load_x(xpre[t * T:(t + 1) * T, :])
            transpose_x()
            ffn_core(s1g, s1u, s1d, ln1g, ln1b)
            transpose_x()
            win_fm(t * T, False)
            win_tm(t * T, t * NB, False, rope_pre)
            for b in range(NB):
                ret_block(b, b * 128, False)
        for t in range(TILES if STAGE >= 2 else 0):
            load_x(xown[t * T:(t + 1) * T, :])
            transpose_x()
            ffn_core(s1g, s1u, s1d, ln1g, ln1b)
            transpose_x()
            win_fm(HALF + t * T, True)
            win_tm(t * T, 16 + t * NB, True, rope_own)
            for b in range(NB):
                ret_block(b, b * 128, True)
            if STAGE < 3:
                continue
            for i in range(NB):
                indexer(t, i)
            if STAGE < 4:
                continue
            attention(t)
            if debug:
                P.dma(OUTQ, d_mixA[t], mixA[:, :, :].rearrange("p h t -> p (h t)"))
                P.dma(OUTQ, d_mixR[t], mixR[:, :, :].rearrange("p h t -> p (h t)"))
            if STAGE < 5:
                continue
            wout_ln2()
            transpose_x()
            ffn_core(s2g, s2u, s2d, ln3g, ln3b)
            P.dma(OUTQ, o_y[t * T:(t + 1) * T, :].rearrange("(b p) d -> p b d", p=128), xtm[:, :, :])
        P.dma(OUTQ, o_S, S[:, :])
        if STAGE >= 6:
            sample_path()

        P.emit()
    return nc


def kernel(**inputs):
    x = np.asarray(inputs["x_prompt"], dtype=np.float32)
    nc = bass.Bass("TRN2", target_bir_lowering=False)
    dbg = bool(globals().get("DEBUG", False))
    build(nc, debug=dbg)
    shared = host_consts()
    for k in ("ffn1_wg", "ffn1_wu", "ffn1_wd", "ffn2_wg", "ffn2_wu", "ffn2_wd", "w_in", "w_out"):
        shared[k] = np.ascontiguousarray(inputs[k][0], dtype=np.float32)
    for k in ("ln1_g", "ln1_b", "ln2_g", "ln2_b", "ln3_g", "ln3_b"):
        shared[k] = np.ascontiguousarray(inputs[k], dtype=np.float32).reshape(D)
    shared["ret_gn_g"] = np.ascontiguousarray(inputs["ret_gn_g"], dtype=np.float32).reshape(512)
    shared["rel_bias"] = np.ascontiguousarray(inputs["rel_bias"], dtype=np.float32)
    shared["cache_kidx"] = np.ascontiguousarray(inputs["cache_kidx"][0], dtype=np.float32).reshape(2560 * 4, 2048)
    shared["cache_k"] = np.ascontiguousarray(inputs["cache_k"][0], dtype=np.float32).reshape(2560 * 128, 512)
    shared["cache_v"] = np.ascontiguousarray(inputs["cache_v"][0], dtype=np.float32).reshape(2560 * 128, 512)
    xsamp = np.asarray(inputs["x_sample"], dtype=np.float32).reshape(32, D)
    ptab = np.ascontiguousarray(inputs["page_table"], dtype=np.int32)
    sret = np.asarray(inputs["state_ret"], dtype=np.float32)[0]
    rope_lo = rope_table(np.arange(0, HALF))
    rope_hi = rope_table(np.arange(HALF, SEQ))
    in_maps = []
    for c in range(8):
        b, h = c // 2, c % 2
        m = dict(shared)
        m["xpre"] = np.ascontiguousarray(x[b, 0:HALF])
        m["xown"] = np.ascontiguousarray(x[b, h * HALF:(h + 1) * HALF])
        m["rope_pre"] = rope_lo
        m["rope_own"] = rope_hi if h else rope_lo
        m["kvd_pre"] = shared["kvd"] if h else np.zeros((128, 512), np.float32)
        m["pbias"] = np.full((128, 1), 0.0 if h else NEG, np.float32)
        xs = np.zeros((128, D), np.float32)
        xs[0:4] = xsamp[c * 4:(c + 1) * 4]
        m["xs"] = xs
        m["pt_s"] = np.ascontiguousarray(ptab[c * 4:(c + 1) * 4])
        m["state_s"] = np.ascontiguousarray(sret[c * 4:(c + 1) * 4])
        in_maps.append(m)
    res = run_bass_kernel_spmd(nc, in_maps, core_ids=list(range(8)))
    r = res.results
    if dbg:
        globals()["LAST"] = r
    y_p = np.zeros((4, SEQ, D), np.float32)
    k_p = np.zeros((1, 4, SEQ, 8, 64), np.float32)
    v_p = np.zeros((1, 4, SEQ, 8, 64), np.float32)
    ki_p = np.zeros((1, 4, SEQ, 64), np.float32)
    s_p = np.zeros((1, 4, 4, 128, 128), np.float32)
    for c in range(8):
        b, h = c // 2, c % 2
        sl = slice(h * HALF, (h + 1) * HALF)
        y_p[b, sl] = r[c]["o_y"]
        k_p[0, b, sl] = r[c]["o_k"].reshape(HALF, 8, 64)
        v_p[0, b, sl] = r[c]["o_v"].reshape(HALF, 8, 64)
        ki_p[0, b, sl] = r[c]["o_ki"].reshape(HALF, 64)
        if h == 1:
            s_p[0, b] = r[c]["o_S"].reshape(128, 4, 128).transpose(1, 0, 2)
    y_s = np.zeros((32, 1, D), np.float32)
    k_s = np.zeros((1, 32, 1, 8, 64), np.float32)
    v_s = np.zeros((1, 32, 1, 8, 64), np.float32)
    ki_s = np.zeros((1, 32, 1, 64), np.float32)
    s_s = np.zeros((1, 32, 4, 128, 128), np.float32)
    for c in range(8):
        sl = slice(c * 4, (c + 1) * 4)
        y_s[sl, 0] = r[c]["o_ys"]
        k_s[0, sl, 0] = r[c]["o_ks"].reshape(4, 8, 64)
        v_s[0, sl, 0] = r[c]["o_vs"].reshape(4, 8, 64)
        ki_s[0, sl, 0] = r[c]["o_kis"]
        s_s[0, sl] = r[c]["o_Ss"].reshape(4, 128, 4, 128).transpose(0, 2, 1, 3)
    return (y_p, y_s, k_p, v_p, ki_p, s_p, k_s, v_s, ki_s, s_s)
```
